# Optimizing a Trainium2 kernel written in Bass

```python
import jax, jax.numpy as jnp
from jax import lax
import numpy as np

D_MODEL = 4096
BATCH = 1
SEQ = 8192
DEPTH = 1
DEC_BATCH = 128
DEC_SEQ = 4
PAST_LEN = 8192
PAGE_SIZE = 128

HEAD_DIM = 64
N_HEADS = D_MODEL // 128
N_KV_HEADS = N_HEADS // 8
N_GROUP = N_HEADS // N_KV_HEADS
WINDOW = 128
BLOCK = 128
ROPE_THETA = 10000.0
CONV_DIM = D_MODEL // 2
CONV_WIDTH = 31
D_FF = ((8 * D_MODEL // 3 + 255) // 256) * 256
Q_DIM = N_HEADS * HEAD_DIM
KV_DIM = N_KV_HEADS * HEAD_DIM
IN_DIM = Q_DIM + 2 * KV_DIM + 2 * CONV_DIM + 2 * D_MODEL
SPLITS = (Q_DIM, Q_DIM + KV_DIM, Q_DIM + 2 * KV_DIM, Q_DIM + 2 * KV_DIM + 2 * CONV_DIM)
EPS = 1e-6
NEG = -1e30

kernel_name = "hybrid_swa_sink_conformer_conv_step"


def _rmsnorm(x, g):
    x32 = x.astype(jnp.float32)
    y = x32 * lax.rsqrt(jnp.mean(x32 * x32, axis=-1, keepdims=True) + EPS) * g.astype(jnp.float32)
    return y.astype(x.dtype)


def _layernorm(x, g, b):
    x32 = x.astype(jnp.float32)
    mu = jnp.mean(x32, axis=-1, keepdims=True)
    var = jnp.mean(jnp.square(x32 - mu), axis=-1, keepdims=True)
    y = (x32 - mu) * lax.rsqrt(var + EPS) * g.astype(jnp.float32) + b.astype(jnp.float32)
    return y.astype(x.dtype)


def _rope(x, pos):
    half = HEAD_DIM // 2
    inv_freq = ROPE_THETA ** (-jnp.arange(half, dtype=jnp.float32) / half)
    ang = pos.astype(jnp.float32)[:, None] * inv_freq[None, :]
    cos = jnp.cos(ang)[None, :, None, :]
    sin = jnp.sin(ang)[None, :, None, :]
    x32 = x.astype(jnp.float32)
    x1, x2 = x32[..., :half], x32[..., half:]
    return jnp.concatenate([x1 * cos - x2 * sin, x2 * cos + x1 * sin], axis=-1).astype(x.dtype)


def _sink_attend(s, mask, sinks, v, eq):
    s = jnp.where(mask, s, NEG)
    sk = sinks.astype(jnp.float32).reshape(N_KV_HEADS, N_GROUP)[..., None]
    m = jnp.maximum(jnp.max(s, axis=-1), sk)
    p = jnp.exp(s - m[..., None])
    p = p / (jnp.sum(p, axis=-1) + jnp.exp(sk - m))[..., None]
    return jnp.einsum(eq, p.astype(v.dtype), v)


def _attn_prompt(q, k, v, sinks):
    B, T = q.shape[0], q.shape[1]
    nb = T // BLOCK
    qb = q.reshape(B, nb, BLOCK, N_KV_HEADS, N_GROUP, HEAD_DIM)
    kb = k.reshape(B, nb, BLOCK, N_KV_HEADS, HEAD_DIM)
    vb = v.reshape(B, nb, BLOCK, N_KV_HEADS, HEAD_DIM)
    pad = ((0, 0), (1, 0), (0, 0), (0, 0), (0, 0))
    kk = jnp.concatenate([jnp.pad(kb, pad)[:, :nb], kb], axis=2)
    vv = jnp.concatenate([jnp.pad(vb, pad)[:, :nb], vb], axis=2)
    s = jnp.einsum('bnqkgd,bnskd->bnkgqs', qb, kk,
                   preferred_element_type=jnp.float32) * (HEAD_DIM ** -0.5)
    n = jnp.arange(nb)[:, None, None]
    qi = jnp.arange(BLOCK)[None, :, None] + BLOCK
    kj = jnp.arange(2 * BLOCK)[None, None, :]
    diff = qi - kj
    mask = (diff >= 0) & (diff < WINDOW) & ((n > 0) | (kj >= BLOCK))
    o = _sink_attend(s, mask[:, None, None], sinks, vv, 'bnkgqs,bnskd->bnqkgd')
    return o.reshape(B, T, Q_DIM)


def _attn_sample(q, k_all, v_all, sinks, qpos, kpos):
    B, T = q.shape[0], q.shape[1]
    qg = q.reshape(B, T, N_KV_HEADS, N_GROUP, HEAD_DIM)
    s = jnp.einsum('btkgd,bskd->bkgts', qg, k_all,
                   preferred_element_type=jnp.float32) * (HEAD_DIM ** -0.5)
    diff = qpos[:, None] - kpos[None, :]
    mask = (diff >= 0) & (diff < WINDOW)
    o = _sink_attend(s, mask, sinks, v_all, 'bkgts,bskd->btkgd')
    return o.reshape(B, T, Q_DIM)


def _dwconv(xs, w, b):
    y = lax.conv_general_dilated(xs, w[:, None, :].astype(xs.dtype), window_strides=(1,),
                                 padding='VALID', dimension_numbers=('NWC', 'WIO', 'NWC'),
                                 feature_group_count=xs.shape[-1])
    return y + b.astype(xs.dtype)


def _forward(x, pos, cache_k, cache_v, state_conv, g_mix_norm, w_in, sinks, w_attn_o, w_dw, b_dw,
             g_conv_ln, b_conv_ln, w_conv_o, w_out, g_ffn_norm, w_ffn_in, w_ffn_out, g_final):
    B, T, _ = x.shape
    ks, vs, cs = [], [], []
    for l in range(DEPTH):
        h = _rmsnorm(x, g_mix_norm[l])
        z = h @ w_in[l]
        q, k, v, u, gate = jnp.split(z, SPLITS, axis=-1)
        q = _rope(q.reshape(B, T, N_HEADS, HEAD_DIM), pos)
        k = _rope(k.reshape(B, T, N_KV_HEADS, HEAD_DIM), pos)
        v = v.reshape(B, T, N_KV_HEADS, HEAD_DIM)
        if cache_k is None:
            o = _attn_prompt(q, k, v, sinks[l])
            k_all, v_all = k, v
            conv_past = jnp.zeros((B, CONV_WIDTH - 1, CONV_DIM), x.dtype)
        else:
            k_all = jnp.concatenate([cache_k[l].astype(k.dtype), k], axis=1)
            v_all = jnp.concatenate([cache_v[l].astype(v.dtype), v], axis=1)
            n_past = cache_k.shape[2]
            kpos = pos[0] - n_past + jnp.arange(n_past + T)
            o = _attn_sample(q, k_all, v_all, sinks[l], pos, kpos)
            conv_past = state_conv[l].astype(x.dtype)
        ks.append(k_all[:, k_all.shape[1] - WINDOW:])
        vs.append(v_all[:, v_all.shape[1] - WINDOW:])
        y_a = o @ w_attn_o[l]
        ua, ub = jnp.split(u, 2, axis=-1)
        cu = ua * jax.nn.sigmoid(ub)
        xs = jnp.concatenate([conv_past, cu], axis=1)
        cs.append(xs[:, xs.shape[1] - (CONV_WIDTH - 1):])
        c = _dwconv(xs, w_dw[l], b_dw[l])
        c = jax.nn.silu(_layernorm(c, g_conv_ln[l], b_conv_ln[l]))
        y_c = c @ w_conv_o[l]
        g_a, g_c = jnp.split(jax.nn.sigmoid(gate), 2, axis=-1)
        x = x + (g_a * y_a + g_c * y_c) @ w_out[l]
        h = _rmsnorm(x, g_ffn_norm[l])
        fg, fu = jnp.split(h @ w_ffn_in[l], 2, axis=-1)
        x = x + (jax.nn.silu(fg) * fu) @ w_ffn_out[l]
    return _rmsnorm(x, g_final), jnp.stack(ks), jnp.stack(vs), jnp.stack(cs)


def setup_inputs(seed: int = 0) -> dict:
    key = jax.random.key(seed)
    ks = jax.random.split(key, 24)
    f32 = jnp.float32
    n_win = min(WINDOW, PAST_LEN)
    nrm = lambda k, shape, scale: jax.random.normal(k, shape, f32) * scale
    return {
        "x_prompt": nrm(ks[0], (BATCH, SEQ, D_MODEL), 1.0),
        "x_sample": nrm(ks[1], (DEC_BATCH, DEC_SEQ, D_MODEL), 1.0),
        "cache_k": nrm(ks[2], (DEPTH, DEC_BATCH, n_win, N_KV_HEADS, HEAD_DIM), 1.0),
        "cache_v": nrm(ks[3], (DEPTH, DEC_BATCH, n_win, N_KV_HEADS, HEAD_DIM), 1.0),
        "state_conv": nrm(ks[4], (DEPTH, DEC_BATCH, CONV_WIDTH - 1, CONV_DIM), 0.5),
        "g_mix_norm": 1.0 + nrm(ks[5], (DEPTH, D_MODEL), 0.01),
        "w_in": nrm(ks[6], (DEPTH, D_MODEL, IN_DIM), D_MODEL ** -0.5),
        "sinks": nrm(ks[7], (DEPTH, N_HEADS), 1.0),
        "w_attn_o": nrm(ks[8], (DEPTH, Q_DIM, D_MODEL), Q_DIM ** -0.5),
        "w_dw": nrm(ks[9], (DEPTH, CONV_WIDTH, CONV_DIM), CONV_WIDTH ** -0.5),
        "b_dw": nrm(ks[10], (DEPTH, CONV_DIM), 0.01),
        "g_conv_ln": 1.0 + nrm(ks[11], (DEPTH, CONV_DIM), 0.01),
        "b_conv_ln": nrm(ks[12], (DEPTH, CONV_DIM), 0.01),
        "w_conv_o": nrm(ks[13], (DEPTH, CONV_DIM, D_MODEL), CONV_DIM ** -0.5),
        "w_out": nrm(ks[14], (DEPTH, D_MODEL, D_MODEL), D_MODEL ** -0.5),
        "g_ffn_norm": 1.0 + nrm(ks[15], (DEPTH, D_MODEL), 0.01),
        "w_ffn_in": nrm(ks[16], (DEPTH, D_MODEL, 2 * D_FF), D_MODEL ** -0.5),
        "w_ffn_out": nrm(ks[17], (DEPTH, D_FF, D_MODEL), D_FF ** -0.5),
        "g_final": 1.0 + nrm(ks[18], (D_MODEL,), 0.01),
    }


def reference(x_prompt, x_sample, cache_k, cache_v, state_conv, g_mix_norm, w_in, sinks, w_attn_o,
              w_dw, b_dw, g_conv_ln, b_conv_ln, w_conv_o, w_out, g_ffn_norm, w_ffn_in, w_ffn_out,
              g_final):
    pos_prompt = jnp.arange(x_prompt.shape[1], dtype=jnp.int32)
    pos_sample = PAST_LEN + jnp.arange(x_sample.shape[1], dtype=jnp.int32)
    y_prompt, k_prompt, v_prompt, conv_prompt = _forward(
        x_prompt, pos_prompt, None, None, None, g_mix_norm, w_in, sinks, w_attn_o, w_dw, b_dw,
        g_conv_ln, b_conv_ln, w_conv_o, w_out, g_ffn_norm, w_ffn_in, w_ffn_out, g_final)
    y_sample, k_sample, v_sample, conv_sample = _forward(
        x_sample, pos_sample, cache_k, cache_v, state_conv, g_mix_norm, w_in, sinks, w_attn_o, w_dw,
        b_dw, g_conv_ln, b_conv_ln, w_conv_o, w_out, g_ffn_norm, w_ffn_in, w_ffn_out, g_final)
    return (y_prompt, y_sample, k_prompt, v_prompt, conv_prompt, k_sample, v_sample, conv_sample)
```

```python
import numpy as np
import concourse.bass as bass
import concourse.mybir as mybir
from concourse.bass_utils import run_bass_kernel_spmd

F32 = mybir.dt.float32
BF16 = mybir.dt.bfloat16
AF = mybir.ActivationFunctionType
ALU = mybir.AluOpType
AX = mybir.AxisListType

EPS = 1e-6
CW = 31
PKC = 8
NRING = 8


class Cfg:
    def __init__(self, D=4096, SEQ=8192, DEC_BATCH=128, NCORES=8):
        self.D = D; self.SEQ = SEQ; self.DEC_BATCH = DEC_BATCH; self.NCORES = NCORES
        self.KC = D // 128
        self.NH = D // 128; self.NKV = self.NH // 8
        self.QD = self.NH * 64; self.QC = self.QD // 128
        self.KVD = self.NKV * 64; self.KVC = self.KVD // 128
        self.CD = D // 2; self.CC = self.CD // 128
        self.DFF = ((8 * D // 3 + 255) // 256) * 256; self.FC = self.DFF // 128
        self.IN_DIM = self.QD + 2 * self.KVD + 2 * self.CD + 2 * D
        self.PT = SEQ // NCORES // 2; self.NB = self.PT // 128
        self.SB = DEC_BATCH // NCORES // 2; self.ST = self.SB * 4
        assert self.ST == 32 or self.ST == 16
        self.NM = self.PT + self.ST
        self.NCOL = 128 + self.NM
        self.NU = 32 + self.NM
        self.NT = self.NB + 1
        self.NK = 128 + self.ST


class Op:
    __slots__ = ("eng", "fn", "deps", "sig", "ev", "waits", "dma", "dsem")

    def __init__(self, eng, fn, dma=False):
        self.eng = eng; self.fn = fn; self.deps = []; self.sig = dma; self.ev = None
        self.waits = []; self.dma = dma; self.dsem = None


class Prog:
    ENGS = ("pe", "act", "dve", "sp", "gq")

    def __init__(self, nsp=16, ngq=12):
        self.ops = []
        self.lastw = {}
        self.readers = {}
        self.dma_pool = {"sp": nsp, "gq": ngq}
        self.dma_rr = {"sp": 0, "gq": 0}
        self.dma_last = {}
        self.phase_key = "PHASE"

    def add(self, eng, fn, reads=(), writes=(), phase=True):
        dma = eng in ("sp", "gq")
        op = Op(eng, fn, dma)
        reads = list(reads); writes = list(writes)
        if phase and eng != "gq":
            reads.append(self.phase_key)
        deps = []
        for k in reads:
            w = self.lastw.get(k)
            if w is not None:
                deps.append(w)
        for k in writes:
            w = self.lastw.get(k)
            if w is not None and (w.dma or w.eng != eng or eng != "pe"):
                deps.append(w)
            rd = self.readers.get(k)
            if rd:
                for r in rd.values():
                    if isinstance(r, list):
                        deps.extend(r)
                    elif r.eng != eng or eng != "pe":
                        deps.append(r)
        if dma:
            idx = self.dma_rr[eng] % self.dma_pool[eng]
            self.dma_rr[eng] += 1
            op.dsem = (eng, idx)
            prev = self.dma_last.get(op.dsem)
            if prev is not None:
                deps.append(prev)
            self.dma_last[op.dsem] = op
        seen = set(); out = []
        for d in deps:
            if id(d) in seen or d is op:
                continue
            seen.add(id(d))
            if (not d.dma) and d.eng == eng and eng == "pe":
                continue
            out.append(d)
        op.deps = out
        for d in out:
            d.sig = True
        for k in writes:
            self.lastw[k] = op
            self.readers[k] = {}
        for k in reads:
            rd = self.readers.setdefault(k, {})
            if dma:
                rd.setdefault("dma", []).append(op)
            else:
                rd[eng] = op
        self.ops.append(op)
        return op

    def barrier(self):
        self.add("dve", ("barrier",), reads=(), writes=(self.phase_key,), phase=False)

    def finalize(self, nc, sems):
        cnt = {}
        known = {e: {} for e in self.ENGS}
        snap = {}
        for op in self.ops:
            kn = known[op.eng]
            for d in op.deps:
                s, v = d.ev
                if kn.get(s, 0) >= v:
                    continue
                op.waits.append((s, v))
                kn[s] = v
                for s2, v2 in snap[id(d)].items():
                    if kn.get(s2, 0) < v2:
                        kn[s2] = v2
            if op.sig:
                s = op.dsem if op.dma else op.eng
                inc = 16 if op.dma else 1
                cnt[s] = cnt.get(s, 0) + inc
                op.ev = (s, cnt[s])
                sn = dict(kn)
                sn[s] = cnt[s]
                snap[id(op)] = sn
        self.maxcnt = cnt

    def emit(self, eng, e, sems):
        for op in self.ops:
            if op.eng != eng:
                continue
            for s, v in op.waits:
                e.wait_ge(sems[s], v)
            fn = op.fn
            if fn is None:
                continue
            if isinstance(fn, tuple):
                ins = e.memset(self._bar_ap, 0.0)
            else:
                ins = fn(e)
            if op.sig:
                s, v = op.ev
                ins.then_inc(sems[s], 16 if op.dma else 1)


def build_program(cfg):
    c = cfg
    D, KC, NH, NKV, QC, KVC, KVD, CD, CC, DFF, FC = c.D, c.KC, c.NH, c.NKV, c.QC, c.KVC, c.KVD, c.CD, c.CC, c.DFF, c.FC
    PT, NB, SB, ST, NM, NCOL, NU, NT, NK = c.PT, c.NB, c.SB, c.ST, c.NM, c.NCOL, c.NU, c.NT, c.NK
    nc = bass.Bass("TRN2", target_bir_lowering=False)
    P = Prog()

    def din(name, shape):
        return nc.dram_tensor(name, list(shape), F32, kind="ExternalInput").ap()

    def dout(name, shape):
        return nc.dram_tensor(name, list(shape), F32, kind="ExternalOutput").ap()

    xp = din("xp", [2 * PT + 128, D]); xs = din("xs", [2 * ST, D])
    ck = din("ck", [2 * SB, 128, KVD]); cv = din("cv", [2 * SB, 128, KVD]); sc = din("sc", [2 * SB, 30, CD])
    w_in = din("w_in", [D, c.IN_DIM]); w_ao = din("w_ao", [c.QD, D]); w_co = din("w_co", [CD, D])
    w_out = din("w_out", [D, D]); w_fi = din("w_fi", [D, 2 * DFF]); w_fo = din("w_fo", [DFF, D])
    NV = 3 * KC + CC * CW + 3 * CC
    vecs = din("vecs", [128, NV])
    rope = din("rope", [2, 2, 128, NCOL])
    pmask = din("pmask", [2, 128, 256]); smask = din("smask", [SB, 32, NK])
    sinkp = din("sinkp", [128, NH]); sinks_ = din("sinks", [32, NKV])
    cmat = din("cmat", [3, 128, 128])
    yp = dout("yp", [2 * PT, D]); ys = dout("ys", [2 * ST, D])
    kpo = dout("kp", [128, KVD]); vpo = dout("vp", [128, KVD]); cpo = dout("cp", [32, CD])
    kso = dout("ks", [2 * SB, 128, KVD]); vso = dout("vs", [2 * SB, 128, KVD]); cso = dout("cs", [2 * SB, 30, CD])

    TOT = 212000 // 4
    arena_cm = nc.sbuf_tensor("arena", [128, TOT], F32)
    arena = arena_cm.__enter__()
    ps_cm = nc.psum_tensor("ps", [128, 8, 512], F32)
    PS = ps_cm.__enter__()

    class Carver:
        def __init__(self, base, limit):
            self.off = base; self.limit = limit

        def take(self, nbytes, dt, shape=None, parts=128):
            nb = (nbytes + 31) // 32 * 32
            o = self.off
            assert o + nb <= self.limit, ("arena overflow", o, nb, self.limit)
            self.off += nb
            ap = arena[0:parts, o // 4:(o + nb) // 4]
            if dt == BF16:
                ap = ap.bitcast(BF16)[:, 0:nbytes // 2]
            else:
                ap = ap[:, 0:nbytes // 4]
            if shape is not None and len(shape) > 1:
                names = " ".join("d%d" % i for i in range(len(shape)))
                kw = {"d%d" % i: s for i, s in enumerate(shape)}
                ap = ap.rearrange("p (%s) -> p %s" % (names, names), **kw)
            return ap

    def tk(cv_, dt, shape, parts=128):
        n = 1
        for s in shape:
            n *= s
        return cv_.take(n * (2 if dt == BF16 else 4), dt, shape, parts)

    top = Carver(0, TOT * 4)
    identf = tk(top, F32, [128]); pswap = tk(top, F32, [128]); onesf = tk(top, F32, [128])
    identb = tk(top, BF16, [128])
    vecsb = tk(top, F32, [NV])
    gmix = vecsb[:, 0:KC]; gffn = vecsb[:, KC:2 * KC]; gfin = vecsb[:, 2 * KC:3 * KC]
    o_ = 3 * KC
    wdw = vecsb[:, o_:o_ + CC * CW].rearrange("p (c k) -> p c k", k=CW); o_ += CC * CW
    bdw = vecsb[:, o_:o_ + CC]; gln = vecsb[:, o_ + CC:o_ + 2 * CC]; bln = vecsb[:, o_ + 2 * CC:o_ + 3 * CC]
    pmaskb = tk(top, F32, [2, 256]); smaskb = tk(top, F32, [SB, NK], parts=32)
    sinkpb = tk(top, F32, [NH]); sinksb = tk(top, F32, [NKV], parts=32)
    small = tk(top, F32, [64])
    barb = tk(top, F32, [8])
    epsb = tk(top, F32, [8])
    P._bar_ap = barb[:, 0:1]
    ring = [tk(top, BF16, [PKC, 256]) for _ in range(NRING)]
    ABASE = top.off

    def region():
        return Carver(ABASE, TOT * 4)

    A = region()
    hT = tk(A, BF16, [KC, NCOL])
    qT = tk(A, BF16, [QC, NM])
    KVOFF = A.off
    kTd = tk(A, BF16, [NKV, NCOL]); kTb_ = tk(A, BF16, [KVC, NCOL])
    NBK = NB + 1
    vpad = tk(A, BF16, [NBK, NKV, 2, 128]); vnpad = tk(A, BF16, [NKV, 2, 128], parts=32)
    if A.off - KVOFF < CC * NM * 2:
        A.off = KVOFF + CC * NM * 2
    CUOFF = A.off
    cuP = tk(A, F32, [CC, 32 + PT])
    cuS = tk(A, F32, [CC, SB, 34])
    cusn = tk(A, F32, [CC, ST])
    SCR0 = A.off
    xt2 = [cuP.rearrange("p c n -> p (c n)")[:, i * D:(i + 1) * D] for i in range(2)] if CC * (32 + PT) >= 2 * D else None
    S = Carver(SCR0, TOT * 4)
    hb2 = [tk(S, BF16, [D]) for _ in range(2)]
    if xt2 is None:
        xt2 = [tk(S, F32, [D]) for _ in range(2)]
    S1 = Carver(SCR0, TOT * 4)
    RSOFF = S1.off
    zf = [tk(S1, F32, [NCOL]) for _ in range(2)]
    t1b = [tk(S1, F32, [NCOL]) for _ in range(1)] * 2
    t2b = [tk(S1, F32, [NCOL]) for _ in range(1)] * 2
    RSEND = S1.off
    S1.off = RSOFF
    sgb = [tk(S1, F32, [2, NU]) for _ in range(2)]
    S1.off = max(S1.off, RSEND)
    ropeb = tk(S1, F32, [2, NCOL])
    stt2 = [tk(S1, F32, [256], parts=120) for _ in range(2)]
    tokf = tk(S1, F32, [128])
    tokc = tk(S1, F32, [512], parts=32)
    kTf = tk(S1, F32, [KVC, NCOL]); vTf = tk(S1, F32, [KVC, NCOL])
    S2 = Carver(SCR0, TOT * 4)
    Sp = [tk(S2, F32, [4, 256]) for _ in range(2)]
    Pf = [tk(S2, F32, [4, 256]) for _ in range(2)]
    Pn = [tk(S2, BF16, [4, 256]) for _ in range(2)]
    PTb = [tk(S2, BF16, [2, 4, 128]) for _ in range(2)]
    S2 = Carver(SCR0, TOT * 4)
    ckf2 = [tk(S2, F32, [KVD]) for _ in range(2)]; cvf2 = [tk(S2, F32, [KVD]) for _ in range(2)]; ckb2 = [tk(S2, BF16, [KVD]) for _ in range(2)]
    cvpad2 = [tk(S2, BF16, [NKV, 2, 128]) for _ in range(2)]
    kTs = tk(S2, BF16, [NKV, NK], parts=64)
    qs = tk(S2, BF16, [SB, NKV, 2, 4, 4], parts=64)
    qtmp = tk(S2, BF16, [QC, ST], parts=64)
    SpS = tk(S2, F32, [NKV, NK], parts=32); PfS = tk(S2, F32, [NKV, NK], parts=32); PnS = tk(S2, BF16, [NKV, NK], parts=32)
    PTc = tk(S2, BF16, [NKV, 32]); PTn = tk(S2, BF16, [NKV, 32], parts=32)
    S3 = Carver(SCR0, TOT * 4)
    cT = tk(Carver(KVOFF, CUOFF), BF16, [CC, NM])
    dgb = [tk(S3, BF16, [16, 128]) for _ in range(2)]
    cub = tk(Carver(KVOFF, CUOFF), BF16, [CC, 32 + PT])
    accS = tk(S3, F32, [CC, SB, 4]); tmpS = [tk(S3, F32, [CC, SB, 4]) for _ in range(2)]
    sqb = [tk(S3, F32, [NM]) for _ in range(2)]
    meanb = tk(S3, F32, [NM]); rstdb = tk(S3, F32, [NM]); msqb = tk(S3, F32, [NM])
    lnt = [tk(S3, F32, [NM]) for _ in range(2)]
    S3END = S3.off
    mT = cuP.rearrange("p c n -> p (c n)").bitcast(BF16)[:, 0:KC * NM].rearrange("p (c n) -> p c n", n=NM) \
        if CC * (32 + PT) * 2 >= KC * NM else None
    assert mT is not None
    SB_ = Carver(SCR0, TOT * 4)
    sgA = tk(SB_, F32, [2, NM]); sgC = tk(SB_, F32, [2, NM]); tAb = tk(SB_, F32, [2, NM]); tCb = tk(SB_, F32, [2, NM])
    Cc = region()
    X = tk(Cc, F32, [KC, NM])
    assert Cc.off <= CUOFF, ("X must not reach cuP(mT)", Cc.off, CUOFF)
    SC_ = Carver(SCR0, TOT * 4)
    xcb = [tk(SC_, F32, [NT, 256]) for _ in range(2)]
    sq2 = [tk(SC_, F32, [NM]) for _ in range(2)]
    rstd2 = tk(SC_, F32, [NM])
    Dd = Carver(Cc.off, TOT * 4)
    ytok = [tk(Carver(Cc.off + i * D * 4, TOT * 4), F32, [D]) for i in range(2)]
    h2T = tk(Dd, BF16, [KC, NM])
    FCH = (FC + 1) // 2
    aT = tk(Dd, BF16, [FCH, NM])
    sgF = [tk(Dd, F32, [2, NM]) for _ in range(2)]
    sq3 = [tk(Dd, F32, [NM]) for _ in range(2)]
    rstd3 = tk(Dd, F32, [NM])

    def bank(b, n0, n1, dt=F32):
        if dt == BF16:
            return PS[:, b, :].bitcast(BF16)[:, n0:n1]
        return PS[:, b, n0:n1]

    def dma(eng, out, in_, reads=(), writes=(), phase=True):
        return P.add(eng, lambda e, o=out, i=in_: e.dma_start(out=o, in_=i), reads, writes, phase)

    ringctr = [0]; slotctr = [0]
    fillers = []

    def drain(n=1):
        for _ in range(n):
            if fillers:
                fillers.pop(0)()

    def dense(wsrc, r0, K, wranges, rhs_fn, rkeys, halves, evac_fn, extra_fn=None):
        nch = sum(w for _, w in wranges) // 128
        npc = (K + PKC - 1) // PKC
        slots = []
        for p in range(npc):
            pk = min(PKC, K - p * PKC)
            sl = ringctr[0] % NRING; ringctr[0] += 1
            slots.append(sl)
            off = 0
            for (c0, wd) in wranges:
                src = wsrc[r0 + p * PKC * 128: r0 + (p * PKC + pk) * 128, c0:c0 + wd].rearrange("(k p) c -> p k c", p=128)
                dma("gq", ring[sl][:, 0:pk, off:off + wd], src, reads=(), writes=(("ring", sl),))
                off += wd
        for ci in range(nch):
            s = slotctr[0] % 3; slotctr[0] += 1
            banks = (2 * s, 2 * s + 1)
            for kc in range(K):
                sl = slots[kc // PKC]
                lhsT = ring[sl][:, kc % PKC, ci * 128:(ci + 1) * 128]
                for hi, (a, b) in enumerate(halves):
                    last = (kc == K - 1) and extra_fn is None
                    P.add("pe", lambda e, o=bank(banks[hi], 0, b - a), l=lhsT, r=rhs_fn(kc, a, b), st=(kc == 0), sp_=last:
                          e.matmul(o, lhsT=l, rhs=r, start=st, stop=sp_),
                          reads=[("ring", sl)] + list(rkeys), writes=[("ps", banks[hi])])
            if extra_fn is not None:
                extra_fn(ci, banks)
            evac_fn(ci, banks)
            drain(1)

    def act(fn, reads, writes):
        return P.add("act", fn, reads, writes)

    def dve(fn, reads, writes):
        return P.add("dve", fn, reads, writes)

    def pe(fn, reads, writes):
        return P.add("pe", fn, reads, writes)

    def rsqrt_chain(src_ap, dst_ap, scale, rk, wk, tmpk):
        dve(lambda e: e.tensor_scalar(out=dst_ap, in0=src_ap, scalar1=scale, scalar2=EPS, op0=ALU.mult, op1=ALU.add), rk, [tmpk])
        act(lambda e: e.activation(out=dst_ap, in_=dst_ap, func=AF.Sqrt), [tmpk], [tmpk + ("b",)])
        dve(lambda e: e.reciprocal(out=dst_ap, in_=dst_ap), [tmpk + ("b",)], wk)

    dma("sp", identf, cmat[0], writes=["c_ident"]); dma("sp", pswap, cmat[1], writes=["c_pswap"]); dma("sp", onesf, cmat[2], writes=["c_ones"])
    dma("sp", vecsb, vecs, writes=["c_vecs"])
    dma("sp", pmaskb, pmask.rearrange("a p n -> p a n"), writes=["c_pmask"])
    dma("sp", smaskb, smask.rearrange("b p n -> p b n"), writes=["c_smask"])
    dma("sp", sinkpb, sinkp, writes=["c_sinkp"]); dma("sp", sinksb, sinks_, writes=["c_sinks"])
    dve(lambda e: e.tensor_copy(out=identb, in_=identf), ["c_ident"], ["c_identb"])
    dve(lambda e: e.memset(epsb, EPS), [], ["c_eps"])
    P.barrier()

    GOFF = c.QD + 2 * KVD + 2 * CD
    UOFF = c.QD + 2 * KVD

    KSTOP = 99
    for hf in range(2):
        if KSTOP < 99 and hf == 1:
            break
        prow0 = hf * PT
        srow0 = hf * ST
        sb0 = hf * SB
        cosT = ropeb[:, 0, :]; sinT = ropeb[:, 1, :]

        tiles = [(prow0 + 128 * i, 128, 128 * i, xp) for i in range(NB + 1)] + [(srow0, ST, 128 + PT, xs)]
        for ti, (r0, nr, col0, src) in enumerate(tiles):
            xt = xt2[ti % 2]
            par = ti % 2
            hb = hb2[par]
            pb = 4 * par
            dma("sp", xt[0:nr, :], src[r0:r0 + nr, :], writes=[("xt", par)])
            act(lambda e, xt=xt, nr=nr, hb=hb, par=par: e.activation(out=hb[0:nr, :], in_=xt[0:nr, :], func=AF.Square, accum_out=small[0:nr, 2 * par:2 * par + 1]),
                [("xt", par)], [("hb", par), ("ss", par)])
            rsqrt_chain(small[0:nr, 2 * par:2 * par + 1], small[0:nr, 2 * par + 1:2 * par + 2], 1.0 / D, [("ss", par)], [("rstd", par)], ("rs_t", par))
            act(lambda e, xt=xt, nr=nr, hb=hb, par=par: e.activation(out=hb[0:nr, :], in_=xt[0:nr, :], func=AF.Copy, scale=small[0:nr, 2 * par + 1:2 * par + 2]),
                [("xt", par), ("rstd", par)], [("hb", par)])
            for kc in range(KC):
                b = pb + kc // 8
                j = kc % 8
                pe(lambda e, o=bank(b, j * 128, j * 128 + nr, BF16), i=hb[0:nr, kc * 128:(kc + 1) * 128], nr=nr:
                   e.transpose(o, i, identb[0:nr, 0:nr]), [("hb", par), "c_identb"], [("ps", b)])
            for g8 in range((KC + 7) // 8):
                n8 = min(8, KC - g8 * 8)
                b = pb + g8
                src_ps = PS[:, b, :].bitcast(BF16).rearrange("p (j n) -> p j n", n=128)[:, 0:n8, 0:nr]
                dve(lambda e, s_=src_ps, g8=g8, n8=n8, nr=nr, col0=col0: e.tensor_tensor(
                    out=hT[:, g8 * 8:g8 * 8 + n8, col0:col0 + nr], in0=s_,
                    in1=gmix[:, g8 * 8:g8 * 8 + n8].unsqueeze(2).to_broadcast([128, n8, nr]), op=ALU.mult),
                    [("ps", b), "c_vecs"], ["hT"])
        P.barrier()

        dma("sp", ropeb, rope[hf].rearrange("a p n -> p a n"), writes=["rope"])
        dve(lambda e: e.memset(vpad.rearrange("p a b c d -> p (a b c d)"), 0.0), [], ["vpad0"] + [("vpad", b_) for b_ in range(NBK + 1)])
        dve(lambda e: e.memset(vnpad.rearrange("p b c d -> p (b c d)"), 0.0), [], ["vnpad0"])
        nst = SB * 30
        tsz = 120 if nst % 120 == 0 else nst
        ststeps = [(t0, c4) for t0 in range(0, nst, tsz) for c4 in range(0, CC, 2)]

        def st_dma(j):
            t0, c4 = ststeps[j]
            b0 = t0 // 30; nb_ = tsz // 30; n4 = min(2, CC - c4)
            dma("sp", stt2[j % 2][0:tsz, 0:n4 * 128], sc[sb0 + b0: sb0 + b0 + nb_, :, c4 * 128:(c4 + n4) * 128].rearrange("b r c -> (b r) c"), writes=[("stt", j % 2)])

        def st_tr(j):
            t0, c4 = ststeps[j]
            b0 = t0 // 30; nb_ = tsz // 30; n4 = min(2, CC - c4)
            for jj in range(n4):
                pe(lambda e, o=bank(6, jj * 128, jj * 128 + tsz), i=stt2[j % 2][0:tsz, jj * 128:(jj + 1) * 128]:
                   e.transpose(o, i, identf[0:tsz, 0:tsz]), [("stt", j % 2), "c_ident"], [("ps", 6)])
            src_ps = PS[:, 6, :].rearrange("p (j n) -> p j n", n=128)[:, 0:n4, 0:tsz].rearrange("p j (b r) -> p j b r", r=30)
            act(lambda e, s_=src_ps, c4=c4, n4=n4, b0=b0, nb_=nb_: e.activation(
                out=cuS[:, c4:c4 + n4, b0:b0 + nb_, 0:30], in_=s_, func=AF.Copy), [("ps", 6)], ["cuS_state"])
        st_dma(0)
        if len(ststeps) > 1:
            st_dma(1)
        for j in range(len(ststeps)):
            fillers.append(lambda j=j: st_tr(j))
            if j + 2 < len(ststeps):
                fillers.append(lambda j=j: st_dma(j + 2))
        dma("sp", cso[sb0:sb0 + SB, 0:26, :], sc[sb0:sb0 + SB, 4:30, :])
        dma("sp", kso[sb0:sb0 + SB, 0:124, :], ck[sb0:sb0 + SB, 4:128, :])
        dma("sp", vso[sb0:sb0 + SB, 0:124, :], cv[sb0:sb0 + SB, 4:128, :])

        HK = [(0, NCOL // 2), (NCOL // 2, NCOL)]
        HM = [(128, 128 + PT // 2), (128 + PT // 2, NCOL)]
        HU = [(96, 96 + NU // 2), (96 + NU // 2, NCOL)]
        rhs_h = lambda kc, a, b: hT[:, kc, a:b]
        ropectr = [0]

        def rope_evac(banks, halves, dst_fn, dkeys, extra_reads=(), defer=False):
            i = ropectr[0] % 2; ropectr[0] += 1
            for hi, (a, b) in enumerate(halves):
                act(lambda e, o=zf[i][:, a:b], s_=bank(banks[hi], 0, b - a): e.activation(out=o, in_=s_, func=AF.Copy),
                    [("ps", banks[hi])], [("zf", i, hi)])

            def rest():
                for hi, (a, b) in enumerate(halves):
                    pe(lambda e, o=bank(6 + hi, 0, b - a), r=zf[i][:, a:b]: e.matmul(o, lhsT=pswap, rhs=r, start=True, stop=True),
                       [("zf", i, hi), "c_pswap"], [("ps", 6 + hi)])
                for hi, (a, b) in enumerate(halves):
                    dve(lambda e, o=t1b[i][:, a:b], z=zf[i][:, a:b], cs=cosT[:, a:b]: e.tensor_tensor(out=o, in0=z, in1=cs, op=ALU.mult),
                        [("zf", i, hi), "rope"], [("t1", i, hi)])
                    dve(lambda e, o=t2b[i][:, a:b], z=bank(6 + hi, 0, b - a), sn=sinT[:, a:b]: e.tensor_tensor(out=o, in0=z, in1=sn, op=ALU.mult),
                        [("ps", 6 + hi), "rope"], [("t2", i, hi)])
                    dve(lambda e, o=dst_fn(a, b), x1=t1b[i][:, a:b], x2=t2b[i][:, a:b]: e.tensor_tensor(out=o, in0=x1, in1=x2, op=ALU.add),
                        [("t1", i, hi), ("t2", i, hi)] + list(extra_reads), dkeys)
            if defer:
                return rest
            rest()
            return None

        def evac_k(ci, banks):
            rope_evac(banks, HK, lambda a, b, ci=ci: kTf[:, ci, a:b], [("kTf", ci)], extra_reads=["kvf_alias"])
            act(lambda e, ci=ci: e.activation(out=kTb_[:, ci, :], in_=kTf[:, ci, :], func=AF.Copy), [("kTf", ci)], [("kTb", ci)])
        for g0 in range(0, KVC, 2):
            n = min(2, KVC - g0)
            dense(w_in, 0, KC, [(c.QD + g0 * 128, n * 128)], rhs_h, ["hT"], HK, lambda ci, bk, g0=g0: evac_k(g0 + ci, bk))
        for kv in range(NKV):
            srcp = kTb_[64 * (kv % 2):64 * (kv % 2) + 64, kv // 2, :]
            for e2 in range(2):
                dma("sp", kTd[64 * e2:64 * e2 + 64, kv, :], srcp, reads=[("kTb", kv // 2)], writes=[("kTd", kv, e2)])

        def evac_v(ci, banks):
            for hi, (a, b) in enumerate(HK):
                act(lambda e, o=vTf[:, ci, a:b], s_=bank(banks[hi], 0, b - a): e.activation(out=o, in_=s_, func=AF.Copy),
                    [("ps", banks[hi]), "kvf_alias"], [("vTf", ci, hi)])
        for g0 in range(0, KVC, 2):
            n = min(2, KVC - g0)
            dense(w_in, 0, KC, [(c.QD + KVD + g0 * 128, n * 128)], rhs_h, ["hT"], HK, lambda ci, bk, g0=g0: evac_v(g0 + ci, bk))
        def v_step(ci, blk):
            nr = 128 if blk < NBK else ST
            col0 = blk * 128
            pe(lambda e, o=bank(7, 0, 128)[0:nr, :], i=vTf[:, ci, col0:col0 + nr]: e.transpose(o, i, identf),
               [("vTf", ci, 0), ("vTf", ci, 1), "c_ident"], [("ps", 7)])
            for e2 in range(2):
                src_ps = PS[0:nr, 7, 0:128].rearrange("p (k d) -> p k d", d=64)
                if blk < NBK:
                    dst = vpad[:, blk, 2 * ci:2 * ci + 2, e2, 64 * e2:64 * e2 + 64]
                else:
                    dst = vnpad[0:nr, 2 * ci:2 * ci + 2, e2, 64 * e2:64 * e2 + 64]
                (act if e2 == 0 else dve)(
                    (lambda e, o=dst, s_=src_ps: e.activation(out=o, in_=s_, func=AF.Copy)) if e2 == 0 else
                    (lambda e, o=dst, s_=src_ps: e.tensor_copy(out=o, in_=s_)),
                    [("ps", 7), "vpad0", "vnpad0"], [("vpad", blk)])
            if (blk == NBK - 1 and hf == 1) or blk == NBK:
                dve(lambda e, nr=nr: e.tensor_copy(out=tokf[0:nr, 0:128], in_=PS[0:nr, 7, 0:128]), [("ps", 7)], ["tokf"])
                if blk == NBK:
                    for b_ in range(SB):
                        dma("sp", vso[sb0 + b_, 124:128, ci * 128:(ci + 1) * 128], tokf[4 * b_:4 * b_ + 4, 0:128], reads=["tokf"])
                else:
                    dma("sp", vpo[:, ci * 128:(ci + 1) * 128], tokf[:, 0:128], reads=["tokf"])

        def k_step(ci, blk):
            nr = 128 if blk < NBK else ST
            col0 = blk * 128
            pe(lambda e, o=bank(7, 0, 128)[0:nr, :], i=kTf[:, ci, col0:col0 + nr]: e.transpose(o, i, identf),
               [("kTf", ci), "c_ident"], [("ps", 7)])
            dve(lambda e, nr=nr: e.tensor_copy(out=tokf[0:nr, 0:128], in_=PS[0:nr, 7, 0:128]), [("ps", 7)], ["tokf"])
            if blk == NBK:
                for b_ in range(SB):
                    dma("sp", kso[sb0 + b_, 124:128, ci * 128:(ci + 1) * 128], tokf[4 * b_:4 * b_ + 4, 0:128], reads=["tokf"])
            else:
                dma("sp", kpo[:, ci * 128:(ci + 1) * 128], tokf[:, 0:128], reads=["tokf"])
        for ci in range(KVC):
            for blk in range(NBK + 1):
                fillers.append(lambda ci=ci, blk=blk: v_step(ci, blk))
            for blk in ([NBK - 1] if hf == 1 else []) + [NBK]:
                fillers.append(lambda ci=ci, blk=blk: k_step(ci, blk))
        P.barrier()

        def cu_step(which, c4):
            n4 = min(4, CC - c4)
            for j in range(n4):
                cc = c4 + j
                if which == 0:
                    src = cuP[:, cc, PT:PT + 32]; nr = 32
                else:
                    src = cusn[:, cc, :]; nr = ST
                pe(lambda e, o=bank(7, j * 128, (j + 1) * 128)[0:nr, :], i=src: e.transpose(o, i, identf),
                   [("cu", cc), ("cusn", cc), "c_ident"], [("ps", 7)])
            act(lambda e, c4=c4, n4=n4, nr=nr: e.activation(out=tokc[0:nr, 0:n4 * 128], in_=PS[0:nr, 7, 0:n4 * 128], func=AF.Copy),
                [("ps", 7)], ["tokc"])
            if which == 0:
                dma("sp", cpo[:, c4 * 128:(c4 + n4) * 128], tokc[0:32, 0:n4 * 128], reads=["tokc"])
            else:
                for b_ in range(SB):
                    dma("sp", cso[sb0 + b_, 26:30, c4 * 128:(c4 + n4) * 128], tokc[4 * b_:4 * b_ + 4, 0:n4 * 128], reads=["tokc"])
        sgctr = [0]
        for g0 in range(0, CC, 2):
            si = sgctr[0] % 2; sgctr[0] += 1

            def evac_ub(ci, banks, si=si):
                for hi, (a, b) in enumerate(HU):
                    act(lambda e, o=sgb[si][:, ci, a - 96:b - 96], s_=bank(banks[hi], 0, b - a): e.activation(out=o, in_=s_, func=AF.Sigmoid),
                        [("ps", banks[hi])], [("sg", si, ci, hi), "kvf_alias"])

            def evac_ua(ci, banks, si=si, g0=g0):
                cc = g0 + ci
                (a0, b0_), (a1, b1) = HU
                n0 = b0_ - a0
                dve(lambda e: e.tensor_tensor(out=cuP[:, cc, 0:n0], in0=bank(banks[0], 0, n0), in1=sgb[si][:, ci, 0:n0], op=ALU.mult),
                    [("ps", banks[0]), ("sg", si, ci, 0)], [("cu", cc)])
                n1p = (32 + PT) - n0
                dve(lambda e: e.tensor_tensor(out=cuP[:, cc, n0:n0 + n1p], in0=bank(banks[1], 0, n1p), in1=sgb[si][:, ci, n0:n0 + n1p], op=ALU.mult),
                    [("ps", banks[1]), ("sg", si, ci, 1)], [("cu", cc)])
                dve(lambda e: e.tensor_tensor(out=cusn[:, cc, :], in0=bank(banks[1], n1p, n1p + ST),
                                              in1=sgb[si][:, ci, n0 + n1p:n0 + n1p + ST], op=ALU.mult),
                    [("ps", banks[1]), ("sg", si, ci, 1)], [("cusn", cc)])
                act(lambda e: e.activation(out=cuS[:, cc, :, 30:34], in_=cusn[:, cc, :].rearrange("p (b t) -> p b t", t=4), func=AF.Copy),
                    [("cusn", cc), "cuS_state"], [("cuSn", cc)])
            dense(w_in, 0, KC, [(UOFF + CD + g0 * 128, 256)], rhs_h, ["hT"], HU, evac_ub)
            dense(w_in, 0, KC, [(UOFF + g0 * 128, 256)], rhs_h, ["hT"], HU, evac_ua)
            if (g0 + 2) % 4 == 0 or g0 + 2 >= CC:
                c4_ = (g0 // 4) * 4
                for which in ([0] if hf == 1 else []) + [1]:
                    fillers.append(lambda which=which, c4_=c4_: cu_step(which, c4_))
        P.barrier()
        qpend = [None]

        def evac_q(qc, banks):
            rest = rope_evac(banks, HM, lambda a, b, qc=qc: qT[:, qc, a - 128:b - 128], [("qT", qc)], defer=True)
            if qpend[0] is not None:
                qpend[0]()
            qpend[0] = rest
        for g0 in range(0, QC, 2):
            dense(w_in, 0, KC, [(g0 * 128, 256)], rhs_h, ["hT"], HM, lambda ci, bk, g0=g0: evac_q(g0 + ci, bk))
        if qpend[0] is not None:
            qpend[0]()

        drain(len(fillers))
        P.barrier()
        if KSTOP <= 1:
            break

        iters = [(blk, kv, hg) for blk in range(1, NB + 1) for kv in range(NKV) for hg in range(2)]

        def emit_qk(n):
            blk, kv, hg = iters[n]
            i2 = n % 2
            bS = (0, 1) if i2 == 0 else (2, 3)
            for g4 in range(4):
                g = hg * 4 + g4; h = kv * 8 + g; qc = h // 2; e2 = h % 2
                pe(lambda e, o=bank(bS[g4 % 2], (g4 // 2) * 256, (g4 // 2) * 256 + 256),
                   l=qT[64 * e2:64 * e2 + 64, qc, (blk - 1) * 128:blk * 128],
                   r=kTd[64 * e2:64 * e2 + 64, kv, (blk - 1) * 128:(blk + 1) * 128]: e.matmul(o, lhsT=l, rhs=r, start=True, stop=True),
                   [("qT", qc), ("kTd", kv, e2)], [("ps", bS[g4 % 2])])

        def emit_A(n):
            blk, kv, hg = iters[n]
            i2 = n % 2
            bS = (0, 1) if i2 == 0 else (2, 3)
            bT = 4 + i2
            bO = 6 + i2
            mk = pmaskb[:, 0 if (hf == 0 and blk == 1) else 1, :]
            psS = PS[:, bS[0]:bS[0] + 2, :].rearrange("p b (g n) -> p (b g) n", n=256)
            dve(lambda e, o=Sp[i2], s_=psS, mk=mk: e.scalar_tensor_tensor(out=o, in0=s_, scalar=0.125, in1=mk.unsqueeze(1).to_broadcast([128, 4, 256]), op0=ALU.mult, op1=ALU.add),
                [("ps", bS[0]), ("ps", bS[1]), "c_pmask"], [("Sp", i2)])
            sm = small[:, 8 + 16 * i2: 8 + 16 * i2 + 16]
            mx = sm[:, 0:4]; ngm = sm[:, 4:8]; rs = sm[:, 8:12]; es = sm[:, 12:16]
            skp = sinkpb[:, kv * 8 + hg * 4: kv * 8 + hg * 4 + 4]
            dve(lambda e, o=mx, s_=Sp[i2]: e.reduce_max(out=o, in_=s_, axis=AX.X), [("Sp", i2)], [("mx", i2)])
            dve(lambda e, o=mx, s_=skp: e.tensor_tensor(out=o, in0=o, in1=s_, op=ALU.max), [("mx", i2), "c_sinkp"], [("mx2", i2)])
            dve(lambda e, o=ngm, s_=mx: e.tensor_scalar(out=o, in0=s_, scalar1=-1.0, scalar2=None, op0=ALU.mult), [("mx2", i2)], [("ngm", i2)])
            dve(lambda e, o=es, s_=skp, n_=ngm: e.tensor_tensor(out=o, in0=s_, in1=n_, op=ALU.add), [("ngm", i2), "c_sinkp"], [("es0", i2)])
            for g4 in range(4):
                act(lambda e, o=Pf[i2][:, g4, :], s_=Sp[i2][:, g4, :], bz=ngm[:, g4:g4 + 1], ao=rs[:, g4:g4 + 1]:
                    e.activation(out=o, in_=s_, func=AF.Exp, bias=bz, accum_out=ao), [("Sp", i2), ("ngm", i2)], [("Pf", i2, g4), ("rs", i2, g4)])
            act(lambda e, o=es: e.activation(out=o, in_=o, func=AF.Exp), [("es0", i2)], [("es", i2)])

        def emit_B(n):
            blk, kv, hg = iters[n]
            i2 = n % 2
            bT = 4 + i2
            bO = 6 + i2
            sm = small[:, 8 + 16 * i2: 8 + 16 * i2 + 16]
            mx = sm[:, 0:4]; ngm = sm[:, 4:8]; rs = sm[:, 8:12]; es = sm[:, 12:16]
            dve(lambda e, o=es, r_=rs: e.tensor_tensor(out=o, in0=o, in1=r_, op=ALU.add), [("es", i2)] + [("rs", i2, g4) for g4 in range(4)], [("den", i2)])
            dve(lambda e, o=es: e.reciprocal(out=o, in_=o), [("den", i2)], [("rinv", i2)])
            dve(lambda e, o=Pn[i2], p_=Pf[i2], r_=es: e.tensor_tensor(out=o, in0=p_, in1=r_.unsqueeze(2).to_broadcast([128, 4, 256]), op=ALU.mult),
                [("rinv", i2)] + [("Pf", i2, g4) for g4 in range(4)], [("Pn", i2)])
            for kb in range(2):
                for g4 in range(4):
                    j = kb * 4 + g4
                    pe(lambda e, o=bank(bT, j * 128, (j + 1) * 128, BF16), i=Pn[i2][:, g4, kb * 128:(kb + 1) * 128]: e.transpose(o, i, identb),
                       [("Pn", i2), "c_identb"], [("ps", bT)])
            act(lambda e, o=PTb[i2].rearrange("p a b c -> p (a b c)"), s_=bank(bT, 0, 1024, BF16): e.activation(out=o, in_=s_, func=AF.Copy),
                [("ps", bT)], [("PT", i2)])
            first = True
            for kb in range(2):
                for e2 in range(2):
                    rhs = PTb[i2][:, kb, 2 * e2:2 * e2 + 2, :].rearrange("p j q -> p (j q)")
                    pe(lambda e, o=bank(bO, 0, 256), l=vpad[:, blk - 1 + kb, kv, e2, :], r=rhs, st=first, sp_=(kb == 1 and e2 == 1):
                       e.matmul(o, lhsT=l, rhs=r, start=st, stop=sp_),
                       [("PT", i2), ("vpad", blk - 1 + kb)], [("ps", bO)])
                    first = False
            qc0 = kv * 4 + hg * 2
            act(lambda e, o=qT[:, qc0:qc0 + 2, (blk - 1) * 128:blk * 128], s_=bank(bO, 0, 256).rearrange("p (j q) -> p j q", q=128):
                e.activation(out=o, in_=s_, func=AF.Copy), [("ps", bO)], [("qT", qc0), ("qT", qc0 + 1)])

        NI = len(iters)
        emit_qk(0)
        if NI > 1:
            emit_qk(1)
        emit_A(0)
        for n in range(NI):
            if n + 2 < NI:
                emit_qk(n + 2)
            if n + 1 < NI:
                emit_A(n + 1)
            emit_B(n)
        P.barrier()
        if KSTOP <= 2:
            break
        dma("sp", qtmp, qT[64:128, :, PT:PT + ST], reads=[("qT", q_) for q_ in range(QC)], writes=["qtmp"])
        for kv in range(NKV):
            act(lambda e, kv=kv: e.activation(out=qs[:, :, kv, 0, :, :].rearrange("p b j t -> p j b t"),
                                              in_=qT[0:64, kv * 4:kv * 4 + 4, PT:PT + ST].rearrange("p j (b t) -> p j b t", t=4), func=AF.Copy),
                [("qT", q_) for q_ in range(QC)], [("qs_e", kv)])
            act(lambda e, kv=kv: e.activation(out=qs[:, :, kv, 1, :, :].rearrange("p b j t -> p j b t"),
                                              in_=qtmp[:, kv * 4:kv * 4 + 4, :].rearrange("p j (b t) -> p j b t", t=4), func=AF.Copy),
                ["qtmp"], [("qs_o", kv)])
        def s_front(b_):
            p_ = b_ % 2
            dma("sp", ckf2[p_], ck[sb0 + b_], writes=[("ckf", p_)]); dma("sp", cvf2[p_], cv[sb0 + b_], writes=[("cvf", p_)])
            act(lambda e: e.activation(out=ckb2[p_], in_=ckf2[p_], func=AF.Copy), [("ckf", p_)], [("ckb", p_)])
            for e2 in range(2):
                dve(lambda e, e2=e2: e.tensor_copy(out=cvpad2[p_][:, :, e2, 64 * e2:64 * e2 + 64], in_=cvf2[p_].rearrange("p (k d) -> p k d", d=64)),
                    [("cvf", p_)], [("cvpad", p_, e2)])
            if b_ < 2:
                dve(lambda e: e.memset(cvpad2[p_][:, :, 0, 64:128], 0.0), [], [("cvpad", p_, 0)])
                dve(lambda e: e.memset(cvpad2[p_][:, :, 1, 0:64], 0.0), [], [("cvpad", p_, 1)])
            for kv in range(NKV):
                pe(lambda e, o=bank(4, kv * 128, (kv + 1) * 128, BF16)[0:64, :], i=ckb2[p_][:, kv * 64:(kv + 1) * 64]: e.transpose(o, i, identb),
                   [("ckb", p_), "c_identb"], [("ps", 4)])
            act(lambda e: e.activation(out=kTs[:, :, 0:128], in_=PS[0:64, 4, :].bitcast(BF16)[:, 0:NKV * 128].rearrange("p (k n) -> p k n", n=128), func=AF.Copy),
                [("ps", 4)], ["kTs_c"])
            dve(lambda e: e.tensor_copy(out=kTs[:, :, 128:NK], in_=kTd[0:64, :, 128 + PT:NCOL]), [("kTd", kv_, 0) for kv_ in range(NKV)], ["kTs_n"])
            for kv in range(NKV):
                pe(lambda e, o=bank(2 * p_, kv * NK, (kv + 1) * NK)[0:32, :] if NKV * NK <= 512 else bank(2 * p_ + kv // 2, (kv % 2) * NK, (kv % 2 + 1) * NK)[0:32, :],
                   l=qs[:, b_, kv].rearrange("p e j t -> p (e j t)"), r=kTs[:, kv, :]: e.matmul(o, lhsT=l, rhs=r, start=True, stop=True),
                   [("qs_e", kv), ("qs_o", kv), "kTs_c", "kTs_n"], [("ps", 2 * p_), ("ps", 2 * p_ + 1)])

        def s_back(b_):
            p_ = b_ % 2
            if NKV * NK <= 512:
                psSs = PS[0:32, 2 * p_, 0:NKV * NK].rearrange("p (k n) -> p k n", n=NK)
            else:
                psSs = PS[0:32, 2 * p_:2 * p_ + 2, 0:2 * NK].rearrange("p b (k n) -> p b k n", n=NK)
            if NKV * NK <= 512:
                dve(lambda e, s_=psSs, b_=b_: e.scalar_tensor_tensor(out=SpS, in0=s_, scalar=0.125, in1=smaskb[:, b_, :].unsqueeze(1).to_broadcast([32, NKV, NK]), op0=ALU.mult, op1=ALU.add),
                    [("ps", 2 * p_), ("ps", 2 * p_ + 1), "c_smask"], ["SpS"])
            else:
                for bk in range(2):
                    dve(lambda e, bk=bk, b_=b_: e.scalar_tensor_tensor(out=SpS[:, 2 * bk:2 * bk + 2, :], in0=PS[0:32, 2 * p_ + bk, 0:2 * NK].rearrange("p (k n) -> p k n", n=NK), scalar=0.125,
                                                                       in1=smaskb[:, b_, :].unsqueeze(1).to_broadcast([32, 2, NK]), op0=ALU.mult, op1=ALU.add),
                        [("ps", 2 * p_ + bk), "c_smask"], ["SpS"])
            sm = small[0:32, 40:40 + 4 * NKV]
            mx = sm[:, 0:NKV]; ngm = sm[:, NKV:2 * NKV]; rs = sm[:, 2 * NKV:3 * NKV]; es = sm[:, 3 * NKV:4 * NKV]
            dve(lambda e: e.reduce_max(out=mx, in_=SpS, axis=AX.X), ["SpS"], ["smx"])
            dve(lambda e: e.tensor_tensor(out=mx, in0=mx, in1=sinksb, op=ALU.max), ["smx", "c_sinks"], ["smx2"])
            dve(lambda e: e.tensor_scalar(out=ngm, in0=mx, scalar1=-1.0, scalar2=None, op0=ALU.mult), ["smx2"], ["sngm"])
            dve(lambda e: e.tensor_tensor(out=es, in0=sinksb, in1=ngm, op=ALU.add), ["sngm", "c_sinks"], ["ses0"])
            for kv in range(NKV):
                act(lambda e, kv=kv: e.activation(out=PfS[:, kv, :], in_=SpS[:, kv, :], func=AF.Exp, bias=ngm[:, kv:kv + 1], accum_out=rs[:, kv:kv + 1]),
                    ["SpS", "sngm"], [("PfS", kv), ("srs", kv)])
            act(lambda e: e.activation(out=es, in_=es, func=AF.Exp), ["ses0"], ["ses"])
            dve(lambda e: e.tensor_tensor(out=es, in0=es, in1=rs, op=ALU.add), ["ses"] + [("srs", kv) for kv in range(NKV)], ["sden"])
            dve(lambda e: e.reciprocal(out=es, in_=es), ["sden"], ["srinv"])
            dve(lambda e: e.tensor_tensor(out=PnS, in0=PfS, in1=es.unsqueeze(2).to_broadcast([32, NKV, NK]), op=ALU.mult),
                ["srinv"] + [("PfS", kv) for kv in range(NKV)], ["PnS"])
            for kv in range(NKV):
                pe(lambda e, o=bank(5, kv * 32, (kv + 1) * 32, BF16), i=PnS[:, kv, 0:128]: e.transpose(o, i, identb[0:32, 0:32]), ["PnS", "c_identb"], [("ps", 5)])
                pe(lambda e, o=bank(5, 512 + kv * 32, 512 + (kv + 1) * 32, BF16)[0:ST, :], i=PnS[:, kv, 128:NK]: e.transpose(o, i, identb[0:32, 0:32]), ["PnS", "c_identb"], [("ps", 5)])
            act(lambda e: e.activation(out=PTc.rearrange("p k n -> p (k n)"), in_=bank(5, 0, NKV * 32, BF16), func=AF.Copy), [("ps", 5)], ["PTc"])
            act(lambda e: e.activation(out=PTn[0:ST].rearrange("p k n -> p (k n)"), in_=bank(5, 512, 512 + NKV * 32, BF16)[0:ST, :], func=AF.Copy), [("ps", 5)], ["PTn"])
            for kv in range(NKV):
                for e2 in range(2):
                    rc = PTc[:, kv, 16 * e2:16 * e2 + 16]
                    rn = PTn[0:ST, kv, 16 * e2:16 * e2 + 16]
                    ob = bank(6, kv * 16, (kv + 1) * 16)
                    pe(lambda e, o=ob, l=cvpad2[p_][:, kv, e2, :], r=rc, st=(e2 == 0): e.matmul(o, lhsT=l, rhs=r, start=st, stop=False),
                       ["PTc", ("cvpad", p_, 0), ("cvpad", p_, 1)], [("ps", 6)])
                    pe(lambda e, o=ob, l=vnpad[0:ST, kv, e2, :], r=rn, sp_=(e2 == 1): e.matmul(o, lhsT=l, rhs=r, start=False, stop=sp_),
                       ["PTn", ("vpad", NBK)], [("ps", 6)])
            act(lambda e, b_=b_: e.activation(out=qT[:, :, PT + 4 * b_:PT + 4 * b_ + 4], in_=PS[:, 6, 0:QC * 4].rearrange("p (c t) -> p c t", t=4), func=AF.Copy),
                [("ps", 6)] + [("qs_e", k_) for k_ in range(NKV)] + ["qtmp"], [("qT", q_) for q_ in range(QC)])
        s_front(0)
        for b_ in range(SB):
            if b_ + 1 < SB:
                s_front(b_ + 1)
            s_back(b_)
        P.barrier()
        if KSTOP <= 3:
            break

        dgctr = [0]
        for cc in range(CC):
            act(lambda e, cc=cc: e.activation(out=cub[:, cc, :], in_=cuP[:, cc, :], func=AF.Copy), [("cu", cc)], [("cub", cc)])
        for cc in range(CC):
            bconv = cc % 6
            for k0 in range(0, CW, 16):
                nk = min(16, CW - k0)
                di = dgctr[0] % 2; dgctr[0] += 1
                dve(lambda e, di=di, cc=cc, k0=k0, nk=nk: e.tensor_tensor(
                    out=dgb[di][:, 0:nk, :], in0=identb.unsqueeze(1).to_broadcast([128, nk, 128]),
                    in1=wdw[:, cc, k0:k0 + nk].unsqueeze(2).to_broadcast([128, nk, 128]), op=ALU.mult),
                    ["c_identb", "c_vecs"], [("dg", di)])
                for k in range(k0, k0 + nk):
                    pe(lambda e, o=bank(bconv, 0, PT), l=dgb[di][:, k - k0, :], r=cub[:, cc, 2 + k:2 + k + PT], st=(k == 0), sp_=(k == CW - 1): e.matmul(o, lhsT=l, rhs=r, start=st, stop=sp_),
                       [("dg", di), ("cub", cc)], [("ps", bconv)])
            act(lambda e, cc=cc, bconv=bconv: e.activation(out=cuP[:, cc, 0:PT], in_=bank(bconv, 0, PT), func=AF.Identity, bias=bdw[:, cc:cc + 1]),
                [("ps", bconv), "c_vecs"], [("cu", cc), ("c", cc)])
        allcu = [("cuSn", cc) for cc in range(CC)] + ["cuS_state"]
        for k in range(CW):
            wb = wdw[:, :, k:k + 1].unsqueeze(3).to_broadcast([128, CC, SB, 4])
            if k == 0:
                dve(lambda e, wb=wb: e.tensor_tensor(out=accS, in0=cuS[:, :, :, 0:4], in1=wb, op=ALU.mult), allcu + ["c_vecs"], ["accS"])
            else:
                ti_ = k % 2
                dve(lambda e, wb=wb, k=k, ti_=ti_: e.tensor_tensor(out=tmpS[ti_], in0=cuS[:, :, :, k:k + 4], in1=wb, op=ALU.mult), allcu + ["c_vecs"], [("tmpS", ti_)])
                dve(lambda e, ti_=ti_: e.tensor_tensor(out=accS, in0=accS, in1=tmpS[ti_], op=ALU.add), ["accS", ("tmpS", ti_)], ["accS"])
        dve(lambda e: e.tensor_tensor(out=cuP[:, :, PT:PT + ST], in0=accS.rearrange("p c b t -> p c (b t)"), in1=bdw.unsqueeze(2).to_broadcast([128, CC, ST]), op=ALU.add),
            ["accS", "c_vecs"] + [("cu", cc) for cc in range(CC)], [("cs_", 0)])
        HN = [(0, PT // 2), (PT // 2, NM)]
        for cc in range(CC):
            i = cc % 2
            act(lambda e, cc=cc, i=i: e.activation(out=sqb[i], in_=cuP[:, cc, 0:NM], func=AF.Square), [("c", cc), ("cs_", 0)], [("sqb", i)])
            for hi, (a, b) in enumerate(HN):
                pe(lambda e, o=bank(hi, 0, b - a), r=cuP[:, cc, a:b], st=(cc == 0), sp_=(cc == CC - 1): e.matmul(o, lhsT=onesf, rhs=r, start=st, stop=sp_),
                   [("c", cc), ("cs_", 0), "c_ones"], [("ps", hi)])
                pe(lambda e, o=bank(2 + hi, 0, b - a), r=sqb[i][:, a:b], st=(cc == 0), sp_=(cc == CC - 1): e.matmul(o, lhsT=onesf, rhs=r, start=st, stop=sp_),
                   [("sqb", i), "c_ones"], [("ps", 2 + hi)])
        for hi, (a, b) in enumerate(HN):
            act(lambda e, a=a, b=b, hi=hi: e.activation(out=meanb[:, a:b], in_=bank(hi, 0, b - a), func=AF.Copy, scale=1.0 / CD), [("ps", hi)], [("mean", hi)])
            act(lambda e, a=a, b=b, hi=hi: e.activation(out=msqb[:, a:b], in_=bank(hi, 0, b - a), func=AF.Square, scale=1.0 / CD), [("ps", hi)], [("msq", hi)])
            dve(lambda e, a=a, b=b, hi=hi: e.scalar_tensor_tensor(out=rstdb[:, a:b], in0=bank(2 + hi, 0, b - a), scalar=1.0 / CD, in1=msqb[:, a:b], op0=ALU.mult, op1=ALU.subtract),
                [("ps", 2 + hi), ("msq", hi)], [("var", hi)])
            rsqrt_chain(rstdb[:, a:b], rstdb[:, a:b], 1.0, [("var", hi)], [("lrstd", hi)], ("lv", hi))
        for cc in range(CC):
            i = cc % 2
            dve(lambda e, cc=cc, i=i: e.tensor_tensor(out=lnt[i], in0=cuP[:, cc, 0:NM], in1=meanb, op=ALU.subtract), [("c", cc), ("cs_", 0), ("mean", 0), ("mean", 1)], [("lnt", i)])
            dve(lambda e, i=i: e.tensor_tensor(out=lnt[i], in0=lnt[i], in1=rstdb, op=ALU.mult), [("lnt", i), ("lrstd", 0), ("lrstd", 1)], [("lnt2", i)])
            act(lambda e, cc=cc, i=i: e.activation(out=cT[:, cc, :], in_=lnt[i], func=AF.Silu, scale=gln[:, cc:cc + 1], bias=bln[:, cc:cc + 1]),
                [("lnt2", i), "c_vecs"], [("cT", cc), ("lnt", i)])
        P.barrier()
        if KSTOP <= 4:
            break

        HM0 = [(0, PT // 2), (PT // 2, NM)]
        rhs_hm = lambda kc, a, b: hT[:, kc, 128 + a:128 + b]
        rhs_o = lambda kc, a, b: qT[:, kc, a:b]
        rhs_c = lambda kc, a, b: cT[:, kc, a:b]
        allq = [("qT", q_) for q_ in range(QC)]; allc = [("cT", cc) for cc in range(CC)]
        for d0 in range(0, KC, 2):
            def ev_sig(dst):
                def f(ci, banks):
                    for hi, (a, b) in enumerate(HM0):
                        act(lambda e, o=dst[:, ci, a:b], s_=bank(banks[hi], 0, b - a): e.activation(out=o, in_=s_, func=AF.Sigmoid),
                            [("ps", banks[hi])], [(id(dst), ci, hi)])
                return f

            def ev_mul(dst, sg):
                def f(ci, banks):
                    for hi, (a, b) in enumerate(HM0):
                        dve(lambda e, o=dst[:, ci, a:b], s_=bank(banks[hi], 0, b - a), g_=sg[:, ci, a:b]: e.tensor_tensor(out=o, in0=s_, in1=g_, op=ALU.mult),
                            [("ps", banks[hi]), (id(sg), ci, hi)], [(id(dst), ci, hi)])
                return f

            def ev_fin(ci, banks, d0=d0):
                ev_mul(tCb, sgC)(ci, banks)
                for hi, (a, b) in enumerate(HM0):
                    dve(lambda e, o=mT[:, d0 + ci, a:b], x1=tAb[:, ci, a:b], x2=tCb[:, ci, a:b]: e.tensor_tensor(out=o, in0=x1, in1=x2, op=ALU.add),
                        [(id(tAb), ci, hi), (id(tCb), ci, hi)], [("mT", d0 + ci)])
            dense(w_in, 0, KC, [(GOFF + d0 * 128, 256)], rhs_hm, ["hT"], HM0, ev_sig(sgA))
            dense(w_ao, 0, QC, [(d0 * 128, 256)], rhs_o, allq, HM0, ev_mul(tAb, sgA))
            dense(w_in, 0, KC, [(GOFF + D + d0 * 128, 256)], rhs_hm, ["hT"], HM0, ev_sig(sgC))
            dense(w_co, 0, CC, [(d0 * 128, 256)], rhs_c, allc, HM0, ev_fin)
        P.barrier()
        if KSTOP <= 5:
            break

        rhs_m = lambda kc, a, b: mT[:, kc, a:b]
        allm = [("mT", k_) for k_ in range(KC)]
        xctr = [0]
        for d0 in range(0, KC, 2):
            xi = xctr[0] % 2; xctr[0] += 1
            dma("sp", xcb[xi][:, 0:NB, :], xp[prow0 + 128: prow0 + 128 + PT, d0 * 128:d0 * 128 + 256].rearrange("(b p) c -> p b c", p=128), writes=[("xcb", xi, 0)])
            dma("sp", xcb[xi][0:ST, NB, :], xs[srow0:srow0 + ST, d0 * 128:d0 * 128 + 256], writes=[("xcb", xi, 1)])

            def extra(ci, banks, xi=xi):
                n0 = PT // 2
                for tt in range(NT):
                    nr = 128 if tt < NB else ST
                    colm = tt * 128
                    hi = 0 if colm < n0 else 1
                    off = colm - (0 if hi == 0 else n0)
                    pe(lambda e, o=bank(banks[hi], off, off + nr), l=xcb[xi][0:nr, tt, ci * 128:(ci + 1) * 128], r=identf[0:nr, 0:nr], sp_=(tt == NT - 1 or tt == NB // 2 - 1):
                       e.matmul(o, lhsT=l, rhs=r, start=False, stop=sp_, skip_group_check=(not sp_)),
                       [("xcb", xi, 0), ("xcb", xi, 1), "c_ident"], [("ps", banks[hi])])

            def ev_x(ci, banks, d0=d0):
                dd = d0 + ci
                for hi, (a, b) in enumerate(HM0):
                    act(lambda e, o=X[:, dd, a:b], s_=bank(banks[hi], 0, b - a): e.activation(out=o, in_=s_, func=AF.Copy), [("ps", banks[hi])], [("X", dd, hi)])
                i = dd % 2
                act(lambda e, dd=dd, i=i: e.activation(out=sq2[i], in_=X[:, dd, :], func=AF.Square), [("X", dd, 0), ("X", dd, 1)], [("sq2", i)])
                for hi, (a, b) in enumerate(HM0):
                    pe(lambda e, o=bank(6 + hi, 0, b - a), r=sq2[i][:, a:b], st=(dd == 0), sp_=(dd == KC - 1): e.matmul(o, lhsT=onesf, rhs=r, start=st, stop=sp_),
                       [("sq2", i), "c_ones"], [("ps", 6 + hi)])
            dense(w_out, 0, KC, [(d0 * 128, 256)], rhs_m, allm, HM0, ev_x, extra_fn=extra)
        for hi, (a, b) in enumerate(HM0):
            rsqrt_chain(bank(6 + hi, 0, b - a), rstd2[:, a:b], 1.0 / D, [("ps", 6 + hi)], [("rstd2", hi)], ("r2", hi))
        P.barrier()
        for dd in range(KC):
            dve(lambda e, dd=dd: e.scalar_tensor_tensor(out=h2T[:, dd, :], in0=X[:, dd, :], scalar=gffn[:, dd:dd + 1], in1=rstd2, op0=ALU.mult, op1=ALU.mult),
                [("X", dd, 0), ("X", dd, 1), ("rstd2", 0), ("rstd2", 1), "c_vecs"], ["h2T"])

        rhs_h2 = lambda kc, a, b: h2T[:, kc, a:b]
        rhs_a = lambda kc, a, b: aT[:, kc, a:b]
        fctr = [0]
        for fh in range(2):
            f0 = fh * FCH
            nf = min(FCH, FC - f0)
            for j0 in range(0, nf, 2):
                nj = min(2, nf - j0)
                si = fctr[0] % 2; fctr[0] += 1

                def ev_g(ci, banks, si=si):
                    for hi, (a, b) in enumerate(HM0):
                        act(lambda e, o=sgF[si][:, ci, a:b], s_=bank(banks[hi], 0, b - a): e.activation(out=o, in_=s_, func=AF.Silu), [("ps", banks[hi])], [("sgF", si, ci, hi)])

                def ev_u(ci, banks, si=si, j0=j0):
                    for hi, (a, b) in enumerate(HM0):
                        dve(lambda e, o=aT[:, j0 + ci, a:b], s_=bank(banks[hi], 0, b - a), g_=sgF[si][:, ci, a:b]: e.tensor_tensor(out=o, in0=s_, in1=g_, op=ALU.mult),
                            [("ps", banks[hi]), ("sgF", si, ci, hi)], [("aT", j0 + ci)])
                dense(w_fi, 0, KC, [((f0 + j0) * 128, nj * 128)], rhs_h2, ["h2T"], HM0, ev_g)
                dense(w_fi, 0, KC, [(DFF + (f0 + j0) * 128, nj * 128)], rhs_h2, ["h2T"], HM0, ev_u)
            alla = [("aT", j) for j in range(nf)]
            for d0 in range(0, KC, 2):
                def ev_o(ci, banks, d0=d0, fh=fh):
                    dd = d0 + ci
                    for hi, (a, b) in enumerate(HM0):
                        dve(lambda e, o=X[:, dd, a:b], s_=bank(banks[hi], 0, b - a): e.tensor_tensor(out=o, in0=s_, in1=o, op=ALU.add),
                            [("ps", banks[hi]), ("X", dd, hi)], [("X", dd, hi)])
                    if fh == 1:
                        i = dd % 2
                        act(lambda e, dd=dd, i=i: e.activation(out=sq3[i], in_=X[:, dd, :], func=AF.Square), [("X", dd, 0), ("X", dd, 1)], [("sq3", i)])
                        for hi, (a, b) in enumerate(HM0):
                            pe(lambda e, o=bank(6 + hi, 0, b - a), r=sq3[i][:, a:b], st=(dd == 0), sp_=(dd == KC - 1): e.matmul(o, lhsT=onesf, rhs=r, start=st, stop=sp_),
                               [("sq3", i), "c_ones"], [("ps", 6 + hi)])
                dense(w_fo, f0 * 128, nf, [(d0 * 128, 256)], rhs_a, alla, HM0, ev_o)
        for hi, (a, b) in enumerate(HM0):
            rsqrt_chain(bank(6 + hi, 0, b - a), rstd3[:, a:b], 1.0 / D, [("ps", 6 + hi)], [("rstd3", hi)], ("r3", hi))
        for dd in range(KC):
            dve(lambda e, dd=dd: e.scalar_tensor_tensor(out=X[:, dd, :], in0=X[:, dd, :], scalar=gfin[:, dd:dd + 1], in1=rstd3, op0=ALU.mult, op1=ALU.mult),
                [("X", dd, 0), ("X", dd, 1), ("rstd3", 0), ("rstd3", 1), "c_vecs"], [("Y", dd)])
        for tt in range(NT):
            nr = 128 if tt < NB else ST
            yi = tt % 2
            for d4 in range(0, KC, 4):
                b = (d4 // 4) % 6
                for j in range(4):
                    pe(lambda e, o=bank(b, j * 128, (j + 1) * 128)[0:nr, :], i=X[:, d4 + j, tt * 128:tt * 128 + nr]: e.transpose(o, i, identf),
                       [("Y", d4 + j), "c_ident"], [("ps", b)])
                if (d4 // 4) % 2 == 0:
                    act(lambda e, b=b, d4=d4, nr=nr, yi=yi: e.activation(out=ytok[yi][0:nr, d4 * 128:(d4 + 4) * 128], in_=PS[0:nr, b, :], func=AF.Copy), [("ps", b)], [("ytok", yi, d4)])
                else:
                    dve(lambda e, b=b, d4=d4, nr=nr, yi=yi: e.tensor_copy(out=ytok[yi][0:nr, d4 * 128:(d4 + 4) * 128], in_=PS[0:nr, b, :]), [("ps", b)], [("ytok", yi, d4)])
            if tt < NB:
                dst = yp[hf * PT + tt * 128: hf * PT + tt * 128 + 128, :]
            else:
                dst = ys[srow0:srow0 + ST, :]
            dma("sp", dst, ytok[yi][0:nr, :], reads=[("ytok", yi, d4) for d4 in range(0, KC, 4)])
        P.barrier()

    fin = P.add("sp", None, reads=[], writes=[P.phase_key], phase=False)
    for k_, last in P.dma_last.items():
        if k_[0] == "sp" and last is not fin and last not in fin.deps:
            fin.deps.append(last)

    names = ["pe", "act", "dve"] + [("sp", i) for i in range(P.dma_pool["sp"])] + [("gq", i) for i in range(P.dma_pool["gq"])]
    sem_cms = [nc.semaphore("s_%s" % (n if isinstance(n, str) else "%s%d" % n)) for n in names]
    sems = {n: cm.__enter__() for n, cm in zip(names, sem_cms)}
    P.finalize(nc, sems)
    with nc.Block() as block:
        @block.tensor
        def _(e):
            P.emit("pe", e, sems)

        @block.scalar
        def _(e):
            P.emit("act", e, sems)

        @block.vector
        def _(e):
            P.emit("dve", e, sems)

        @block.sync
        def _(e):
            P.emit("sp", e, sems)

        @block.gpsimd
        def _(e):
            P.emit("gq", e, sems)
    for cm in reversed(sem_cms):
        cm.__exit__(None, None, None)
    ps_cm.__exit__(None, None, None)
    arena_cm.__exit__(None, None, None)
    return nc, P


def _consts(cfg):
    ident = np.eye(128, dtype=np.float32)
    psw = np.zeros((128, 128), np.float32)
    for m in range(128):
        k = m + 32 if (m % 64) < 32 else m - 32
        psw[k, m] = 1.0
    ones = np.ones((128, 128), np.float32)
    return np.stack([ident, psw, ones])


def _rope_tables(pos):
    half = 32
    inv = (np.float32(10000.0) ** (-np.arange(half, dtype=np.float32) / np.float32(half))).astype(np.float32)
    ang = pos.astype(np.float32)[None, :] * inv[:, None]
    cs = np.cos(ang).astype(np.float32); sn = np.sin(ang).astype(np.float32)
    cosT = np.concatenate([cs, cs, cs, cs], 0)
    sinT = np.concatenate([-sn, sn, -sn, sn], 0)
    return np.stack([cosT, sinT])


def make_in_maps(cfg, inp):
    c = cfg
    x_prompt = np.asarray(inp["x_prompt"], np.float32)[0]
    x_sample = np.asarray(inp["x_sample"], np.float32).reshape(c.DEC_BATCH * 4, c.D)
    ckv = np.asarray(inp["cache_k"], np.float32)[0].reshape(c.DEC_BATCH, 128, c.KVD)
    cvv = np.asarray(inp["cache_v"], np.float32)[0].reshape(c.DEC_BATCH, 128, c.KVD)
    scv = np.asarray(inp["state_conv"], np.float32)[0]
    t = lambda v, n: np.ascontiguousarray(np.asarray(v, np.float32).reshape(n, 128).T)
    wdw = np.asarray(inp["w_dw"], np.float32)[0]
    wdwT = np.ascontiguousarray(wdw.reshape(CW, c.CC, 128).transpose(2, 1, 0)).reshape(128, c.CC * CW)
    vecs = np.concatenate([t(inp["g_mix_norm"], c.KC), t(inp["g_ffn_norm"], c.KC), t(inp["g_final"], c.KC), wdwT,
                           t(inp["b_dw"], c.CC), t(inp["g_conv_ln"], c.CC), t(inp["b_conv_ln"], c.CC)], 1).astype(np.float32)
    sinks = np.asarray(inp["sinks"], np.float32)[0]
    perm = np.arange(c.NH).reshape(-1, 2, 2).transpose(0, 2, 1).reshape(-1)
    sinkp = np.ascontiguousarray(np.broadcast_to(sinks[perm][None, :], (128, c.NH)))
    sk = sinks.reshape(c.NKV, 4, 2)
    sinkS = np.ascontiguousarray(np.repeat(sk.transpose(2, 1, 0).reshape(8, c.NKV), 4, axis=0))
    qi = np.arange(128)[:, None] + 128; kj = np.arange(256)[None, :]
    diff = qi - kj
    m_norm = np.where((diff >= 0) & (diff < 128), 0.0, -1e30).astype(np.float32)
    m_first = np.where((diff >= 0) & (diff < 128) & (kj >= 128), 0.0, -1e30).astype(np.float32)
    smask = np.full((c.SB, 32, c.NK), -1e30, np.float32)
    for b in range(c.SB):
        for g in range(8):
            for tq in range(4):
                r = g * 4 + tq
                smask[b, r, tq + 1:128] = 0.0
                for t2 in range(tq + 1):
                    smask[b, r, 128 + 4 * b + t2] = 0.0
    cm = _consts(c)
    W = {k: np.asarray(inp[k], np.float32)[0] for k in ("w_in", "w_attn_o", "w_conv_o", "w_out", "w_ffn_in", "w_ffn_out")}
    PTC = 2 * c.PT
    maps = []
    for core in range(c.NCORES):
        xpc = np.zeros((PTC + 128, c.D), np.float32)
        lo = core * PTC - 128
        if lo >= 0:
            xpc[:] = x_prompt[lo:lo + PTC + 128]
        else:
            xpc[128:] = x_prompt[0:PTC]
        ropes = []
        for hf in range(2):
            p0 = core * PTC + hf * c.PT
            pos = np.concatenate([np.arange(p0 - 128, p0 + c.PT), np.tile(np.arange(c.SEQ if False else PAST_LEN_, PAST_LEN_ + 4), c.SB)])
            ropes.append(_rope_tables(pos))
        b0 = core * 2 * c.SB
        maps.append({
            "xp": xpc, "xs": np.ascontiguousarray(x_sample[b0 * 4:(b0 + 2 * c.SB) * 4]),
            "ck": np.ascontiguousarray(ckv[b0:b0 + 2 * c.SB]), "cv": np.ascontiguousarray(cvv[b0:b0 + 2 * c.SB]),
            "sc": np.ascontiguousarray(scv[b0:b0 + 2 * c.SB]),
            "w_in": W["w_in"], "w_ao": W["w_attn_o"], "w_co": W["w_conv_o"], "w_out": W["w_out"], "w_fi": W["w_ffn_in"], "w_fo": W["w_ffn_out"],
            "vecs": vecs, "rope": np.stack(ropes).astype(np.float32),
            "pmask": np.stack([m_first if core == 0 else m_norm, m_norm]), "smask": smask,
            "sinkp": sinkp, "sinks": sinkS, "cmat": cm,
        })
    return maps


PAST_LEN_ = 8192
_CACHE = {}


def run(cfg, inp, trace=False, cores=None):
    key = (cfg.D, cfg.SEQ, cfg.DEC_BATCH, cfg.NCORES)
    if key not in _CACHE:
        _CACHE[key] = build_program(cfg)[0]
    nc = _CACHE[key]
    maps = make_in_maps(cfg, inp)
    if cores is not None:
        maps = [maps[i] for i in cores]
        res = run_bass_kernel_spmd(nc, maps, core_ids=list(range(len(maps))))
        return None, res
    res = run_bass_kernel_spmd(nc, maps, core_ids=list(range(cfg.NCORES)), **({"trace": True} if trace else {}))
    R = res.results
    c = cfg
    y_prompt = np.concatenate([r["yp"] for r in R], 0)[None]
    y_sample = np.concatenate([r["ys"] for r in R], 0).reshape(c.DEC_BATCH, 4, c.D)
    last = R[-1]
    k_prompt = last["kp"].reshape(1, 1, 128, c.NKV, 64)
    v_prompt = last["vp"].reshape(1, 1, 128, c.NKV, 64)
    conv_prompt = last["cp"][2:32].reshape(1, 1, 30, c.CD)
    k_sample = np.concatenate([r["ks"] for r in R], 0).reshape(1, c.DEC_BATCH, 128, c.NKV, 64)
    v_sample = np.concatenate([r["vs"] for r in R], 0).reshape(1, c.DEC_BATCH, 128, c.NKV, 64)
    conv_sample = np.concatenate([r["cs"] for r in R], 0).reshape(1, c.DEC_BATCH, 30, c.CD)
    outs = (y_prompt, y_sample, k_prompt, v_prompt, conv_prompt, k_sample, v_sample, conv_sample)
    return tuple(np.ascontiguousarray(o, dtype=np.float32) for o in outs), res


def kernel(**inputs):
    cfg = Cfg()
    outs, _ = run(cfg, inputs)
    return outs
```

```python
import numpy as np
import concourse.bass as bass
import concourse.mybir as mybir
from concourse.bass_utils import run_bass_kernel_spmd

F32 = mybir.dt.float32
BF16 = mybir.dt.bfloat16
AF = mybir.ActivationFunctionType
ALU = mybir.AluOpType
AX = mybir.AxisListType

EPS = 1e-6
CW = 31
PKC = 8
NRING = 8


class Cfg:
    def __init__(self, D=4096, SEQ=8192, DEC_BATCH=128, NCORES=8):
        self.D = D; self.SEQ = SEQ; self.DEC_BATCH = DEC_BATCH; self.NCORES = NCORES
        self.KC = D // 128
        self.NH = D // 128; self.NKV = self.NH // 8
        self.QD = self.NH * 64; self.QC = self.QD // 128
        self.KVD = self.NKV * 64; self.KVC = self.KVD // 128
        self.CD = D // 2; self.CC = self.CD // 128
        self.DFF = ((8 * D // 3 + 255) // 256) * 256; self.FC = self.DFF // 128
        self.IN_DIM = self.QD + 2 * self.KVD + 2 * self.CD + 2 * D
        self.PT = SEQ // NCORES // 2; self.NB = self.PT // 128
        self.SB = DEC_BATCH // NCORES // 2; self.ST = self.SB * 4
        assert self.ST == 32 or self.ST == 16
        self.NM = self.PT + self.ST
        self.NCOL = 128 + self.NM
        self.NU = 32 + self.NM
        self.NT = self.NB + 1
        self.NK = 128 + self.ST


class Op:
    __slots__ = ("eng", "fn", "deps", "sig", "ev", "waits", "dma", "dsem")

    def __init__(self, eng, fn, dma=False):
        self.eng = eng; self.fn = fn; self.deps = []; self.sig = dma; self.ev = None
        self.waits = []; self.dma = dma; self.dsem = None


class Prog:
    ENGS = ("pe", "act", "dve", "sp", "gq")

    def __init__(self, nsp=16, ngq=12):
        self.ops = []
        self.lastw = {}
        self.readers = {}
        self.dma_pool = {"sp": nsp, "gq": ngq}
        self.dma_rr = {"sp": 0, "gq": 0}
        self.dma_last = {}
        self.phase_key = "PHASE"

    def add(self, eng, fn, reads=(), writes=(), phase=True):
        dma = eng in ("sp", "gq")
        op = Op(eng, fn, dma)
        reads = list(reads); writes = list(writes)
        if phase and eng != "gq":
            reads.append(self.phase_key)
        deps = []
        for k in reads:
            w = self.lastw.get(k)
            if w is not None:
                deps.append(w)
        for k in writes:
            w = self.lastw.get(k)
            if w is not None and (w.dma or w.eng != eng or eng != "pe"):
                deps.append(w)
            rd = self.readers.get(k)
            if rd:
                for r in rd.values():
                    if isinstance(r, list):
                        deps.extend(r)
                    elif r.eng != eng or eng != "pe":
                        deps.append(r)
        if dma:
            idx = self.dma_rr[eng] % self.dma_pool[eng]
            self.dma_rr[eng] += 1
            op.dsem = (eng, idx)
            prev = self.dma_last.get(op.dsem)
            if prev is not None:
                deps.append(prev)
            self.dma_last[op.dsem] = op
        seen = set(); out = []
        for d in deps:
            if id(d) in seen or d is op:
                continue
            seen.add(id(d))
            if (not d.dma) and d.eng == eng and eng == "pe":
                continue
            out.append(d)
        op.deps = out
        for d in out:
            d.sig = True
        for k in writes:
            self.lastw[k] = op
            self.readers[k] = {}
        for k in reads:
            rd = self.readers.setdefault(k, {})
            if dma:
                rd.setdefault("dma", []).append(op)
            else:
                rd[eng] = op
        self.ops.append(op)
        return op

    def barrier(self):
        self.add("dve", ("barrier",), reads=(), writes=(self.phase_key,), phase=False)

    def finalize(self, nc, sems):
        cnt = {}
        known = {e: {} for e in self.ENGS}
        snap = {}
        for op in self.ops:
            kn = known[op.eng]
            for d in op.deps:
                s, v = d.ev
                if kn.get(s, 0) >= v:
                    continue
                op.waits.append((s, v))
                kn[s] = v
                for s2, v2 in snap[id(d)].items():
                    if kn.get(s2, 0) < v2:
                        kn[s2] = v2
            if op.sig:
                s = op.dsem if op.dma else op.eng
                inc = 16 if op.dma else 1
                cnt[s] = cnt.get(s, 0) + inc
                op.ev = (s, cnt[s])
                sn = dict(kn)
                sn[s] = cnt[s]
                snap[id(op)] = sn
        self.maxcnt = cnt

    def emit(self, eng, e, sems):
        for op in self.ops:
            if op.eng != eng:
                continue
            for s, v in op.waits:
                e.wait_ge(sems[s], v)
            fn = op.fn
            if fn is None:
                continue
            if isinstance(fn, tuple):
                ins = e.memset(self._bar_ap, 0.0)
            else:
                ins = fn(e)
            if op.sig:
                s, v = op.ev
                ins.then_inc(sems[s], 16 if op.dma else 1)


def build_program(cfg):
    c = cfg
    D, KC, NH, NKV, QC, KVC, KVD, CD, CC, DFF, FC = c.D, c.KC, c.NH, c.NKV, c.QC, c.KVC, c.KVD, c.CD, c.CC, c.DFF, c.FC
    PT, NB, SB, ST, NM, NCOL, NU, NT, NK = c.PT, c.NB, c.SB, c.ST, c.NM, c.NCOL, c.NU, c.NT, c.NK
    nc = bass.Bass("TRN2", target_bir_lowering=False)
    P = Prog()

    def din(name, shape):
        return nc.dram_tensor(name, list(shape), F32, kind="ExternalInput").ap()

    def dout(name, shape):
        return nc.dram_tensor(name, list(shape), F32, kind="ExternalOutput").ap()

    xp = din("xp", [2 * PT + 128, D]); xs = din("xs", [2 * ST, D])
    ck = din("ck", [2 * SB, 128, KVD]); cv = din("cv", [2 * SB, 128, KVD]); sc = din("sc", [2 * SB, 30, CD])
    w_in = din("w_in", [D, c.IN_DIM]); w_ao = din("w_ao", [c.QD, D]); w_co = din("w_co", [CD, D])
    w_out = din("w_out", [D, D]); w_fi = din("w_fi", [D, 2 * DFF]); w_fo = din("w_fo", [DFF, D])
    NV = 3 * KC + CC * CW + 3 * CC
    vecs = din("vecs", [128, NV])
    rope = din("rope", [2, 2, 128, NCOL])
    pmask = din("pmask", [2, 128, 256]); smask = din("smask", [SB, 32, NK])
    sinkp = din("sinkp", [128, NH]); sinks_ = din("sinks", [32, NKV])
    cmat = din("cmat", [3, 128, 128])
    yp = dout("yp", [2 * PT, D]); ys = dout("ys", [2 * ST, D])
    kpo = dout("kp", [128, KVD]); vpo = dout("vp", [128, KVD]); cpo = dout("cp", [32, CD])
    kso = dout("ks", [2 * SB, 128, KVD]); vso = dout("vs", [2 * SB, 128, KVD]); cso = dout("cs", [2 * SB, 30, CD])

    TOT = 212000 // 4
    arena_cm = nc.sbuf_tensor("arena", [128, TOT], F32)
    arena = arena_cm.__enter__()
    ps_cm = nc.psum_tensor("ps", [128, 8, 512], F32)
    PS = ps_cm.__enter__()

    class Carver:
        def __init__(self, base, limit):
            self.off = base; self.limit = limit

        def take(self, nbytes, dt, shape=None, parts=128):
            nb = (nbytes + 31) // 32 * 32
            o = self.off
            assert o + nb <= self.limit, ("arena overflow", o, nb, self.limit)
            self.off += nb
            ap = arena[0:parts, o // 4:(o + nb) // 4]
            if dt == BF16:
                ap = ap.bitcast(BF16)[:, 0:nbytes // 2]
            else:
                ap = ap[:, 0:nbytes // 4]
            if shape is not None and len(shape) > 1:
                names = " ".join("d%d" % i for i in range(len(shape)))
                kw = {"d%d" % i: s for i, s in enumerate(shape)}
                ap = ap.rearrange("p (%s) -> p %s" % (names, names), **kw)
            return ap

    def tk(cv_, dt, shape, parts=128):
        n = 1
        for s in shape:
            n *= s
        return cv_.take(n * (2 if dt == BF16 else 4), dt, shape, parts)

    top = Carver(0, TOT * 4)
    identf = tk(top, F32, [128]); pswap = tk(top, F32, [128]); onesf = tk(top, F32, [128])
    identb = tk(top, BF16, [128])
    vecsb = tk(top, F32, [NV])
    gmix = vecsb[:, 0:KC]; gffn = vecsb[:, KC:2 * KC]; gfin = vecsb[:, 2 * KC:3 * KC]
    o_ = 3 * KC
    wdw = vecsb[:, o_:o_ + CC * CW].rearrange("p (c k) -> p c k", k=CW); o_ += CC * CW
    bdw = vecsb[:, o_:o_ + CC]; gln = vecsb[:, o_ + CC:o_ + 2 * CC]; bln = vecsb[:, o_ + 2 * CC:o_ + 3 * CC]
    pmaskb = tk(top, F32, [2, 256]); smaskb = tk(top, F32, [SB, NK], parts=32)
    sinkpb = tk(top, F32, [NH]); sinksb = tk(top, F32, [NKV], parts=32)
    small = tk(top, F32, [64])
    barb = tk(top, F32, [8])
    epsb = tk(top, F32, [8])
    P._bar_ap = barb[:, 0:1]
    ring = [tk(top, BF16, [PKC, 256]) for _ in range(NRING)]
    ABASE = top.off

    def region():
        return Carver(ABASE, TOT * 4)

    A = region()
    hT = tk(A, BF16, [KC, NCOL])
    qT = tk(A, BF16, [QC, NM])
    KVOFF = A.off
    kTd = tk(A, BF16, [NKV, NCOL]); kTb_ = tk(A, BF16, [KVC, NCOL])
    NBK = NB + 1
    vpad = tk(A, BF16, [NBK, NKV, 2, 128]); vnpad = tk(A, BF16, [NKV, 2, 128], parts=32)
    if A.off - KVOFF < CC * NM * 2:
        A.off = KVOFF + CC * NM * 2
    CUOFF = A.off
    cuP = tk(A, F32, [CC, 32 + PT])
    cuS = tk(A, F32, [CC, SB, 34])
    cusn = tk(A, F32, [CC, ST])
    SCR0 = A.off
    xt2 = [cuP.rearrange("p c n -> p (c n)")[:, i * D:(i + 1) * D] for i in range(2)] if CC * (32 + PT) >= 2 * D else None
    S = Carver(SCR0, TOT * 4)
    hb2 = [tk(S, BF16, [D]) for _ in range(2)]
    if xt2 is None:
        xt2 = [tk(S, F32, [D]) for _ in range(2)]
    S1 = Carver(SCR0, TOT * 4)
    RSOFF = S1.off
    zf = [tk(S1, F32, [NCOL]) for _ in range(2)]
    t1b = [tk(S1, F32, [NCOL]) for _ in range(1)] * 2
    t2b = [tk(S1, F32, [NCOL]) for _ in range(1)] * 2
    RSEND = S1.off
    S1.off = RSOFF
    sgb = [tk(S1, F32, [2, NU]) for _ in range(2)]
    S1.off = max(S1.off, RSEND)
    ropeb = tk(S1, F32, [2, NCOL])
    stt2 = [tk(S1, F32, [256], parts=120) for _ in range(2)]
    tokf = tk(S1, F32, [128])
    tokc = tk(S1, F32, [512], parts=32)
    kTf = tk(S1, F32, [KVC, NCOL]); vTf = tk(S1, F32, [KVC, NCOL])
    S2 = Carver(SCR0, TOT * 4)
    Sp = [tk(S2, F32, [4, 256]) for _ in range(2)]
    Pf = [tk(S2, F32, [4, 256]) for _ in range(2)]
    Pn = [tk(S2, BF16, [4, 256]) for _ in range(2)]
    PTb = [tk(S2, BF16, [2, 4, 128]) for _ in range(2)]
    S2 = Carver(SCR0, TOT * 4)
    ckf2 = [tk(S2, F32, [KVD]) for _ in range(2)]; cvf2 = [tk(S2, F32, [KVD]) for _ in range(2)]; ckb2 = [tk(S2, BF16, [KVD]) for _ in range(2)]
    cvpad2 = [tk(S2, BF16, [NKV, 2, 128]) for _ in range(2)]
    kTs = tk(S2, BF16, [NKV, NK], parts=64)
    qs = tk(S2, BF16, [SB, NKV, 2, 4, 4], parts=64)
    qtmp = tk(S2, BF16, [QC, ST], parts=64)
    SpS = tk(S2, F32, [NKV, NK], parts=32); PfS = tk(S2, F32, [NKV, NK], parts=32); PnS = tk(S2, BF16, [NKV, NK], parts=32)
    PTc = tk(S2, BF16, [NKV, 32]); PTn = tk(S2, BF16, [NKV, 32], parts=32)
    S3 = Carver(SCR0, TOT * 4)
    cT = tk(Carver(KVOFF, CUOFF), BF16, [CC, NM])
    dgb = [tk(S3, BF16, [16, 128]) for _ in range(2)]
    cub = tk(Carver(KVOFF, CUOFF), BF16, [CC, 32 + PT])
    accS = tk(S3, F32, [CC, SB, 4]); tmpS = [tk(S3, F32, [CC, SB, 4]) for _ in range(2)]
    sqb = [tk(S3, F32, [NM]) for _ in range(2)]
    meanb = tk(S3, F32, [NM]); rstdb = tk(S3, F32, [NM]); msqb = tk(S3, F32, [NM])
    lnt = [tk(S3, F32, [NM]) for _ in range(2)]
    S3END = S3.off
    mT = cuP.rearrange("p c n -> p (c n)").bitcast(BF16)[:, 0:KC * NM].rearrange("p (c n) -> p c n", n=NM) \
        if CC * (32 + PT) * 2 >= KC * NM else None
    assert mT is not None
    SB_ = Carver(SCR0, TOT * 4)
    sgA = tk(SB_, F32, [2, NM]); sgC = tk(SB_, F32, [2, NM]); tAb = tk(SB_, F32, [2, NM]); tCb = tk(SB_, F32, [2, NM])
    Cc = region()
    X = tk(Cc, F32, [KC, NM])
    assert Cc.off <= CUOFF, ("X must not reach cuP(mT)", Cc.off, CUOFF)
    SC_ = Carver(SCR0, TOT * 4)
    xcb = [tk(SC_, F32, [NT, 256]) for _ in range(2)]
    sq2 = [tk(SC_, F32, [NM]) for _ in range(2)]
    rstd2 = tk(SC_, F32, [NM])
    Dd = Carver(Cc.off, TOT * 4)
    ytok = [tk(Carver(Cc.off + i * D * 4, TOT * 4), F32, [D]) for i in range(2)]
    h2T = tk(Dd, BF16, [KC, NM])
    FCH = (FC + 1) // 2
    aT = tk(Dd, BF16, [FCH, NM])
    sgF = [tk(Dd, F32, [2, NM]) for _ in range(2)]
    sq3 = [tk(Dd, F32, [NM]) for _ in range(2)]
    rstd3 = tk(Dd, F32, [NM])

    def bank(b, n0, n1, dt=F32):
        if dt == BF16:
            return PS[:, b, :].bitcast(BF16)[:, n0:n1]
        return PS[:, b, n0:n1]

    def dma(eng, out, in_, reads=(), writes=(), phase=True):
        return P.add(eng, lambda e, o=out, i=in_: e.dma_start(out=o, in_=i), reads, writes, phase)

    ringctr = [0]; slotctr = [0]
    fillers = []

    def drain(n=1):
        for _ in range(n):
            if fillers:
                fillers.pop(0)()

    def dense(wsrc, r0, K, wranges, rhs_fn, rkeys, halves, evac_fn, extra_fn=None):
        nch = sum(w for _, w in wranges) // 128
        npc = (K + PKC - 1) // PKC
        slots = []
        for p in range(npc):
            pk = min(PKC, K - p * PKC)
            sl = ringctr[0] % NRING; ringctr[0] += 1
            slots.append(sl)
            off = 0
            for (c0, wd) in wranges:
                src = wsrc[r0 + p * PKC * 128: r0 + (p * PKC + pk) * 128, c0:c0 + wd].rearrange("(k p) c -> p k c", p=128)
                dma("gq", ring[sl][:, 0:pk, off:off + wd], src, reads=(["x_first"] if ringctr[0] <= NRING else ()), writes=(("ring", sl),))
                off += wd
        for ci in range(nch):
            s = slotctr[0] % 3; slotctr[0] += 1
            banks = (2 * s, 2 * s + 1)
            for kc in range(K):
                sl = slots[kc // PKC]
                lhsT = ring[sl][:, kc % PKC, ci * 128:(ci + 1) * 128]
                for hi, (a, b) in enumerate(halves):
                    last = (kc == K - 1) and extra_fn is None
                    P.add("pe", lambda e, o=bank(banks[hi], 0, b - a), l=lhsT, r=rhs_fn(kc, a, b), st=(kc == 0), sp_=last:
                          e.matmul(o, lhsT=l, rhs=r, start=st, stop=sp_),
                          reads=[("ring", sl)] + list(rkeys), writes=[("ps", banks[hi])])
            if extra_fn is not None:
                extra_fn(ci, banks)
            evac_fn(ci, banks)
            drain(1)

    def act(fn, reads, writes):
        return P.add("act", fn, reads, writes)

    def dve(fn, reads, writes):
        return P.add("dve", fn, reads, writes)

    def pe(fn, reads, writes):
        return P.add("pe", fn, reads, writes)

    def rsqrt_chain(src_ap, dst_ap, scale, rk, wk, tmpk):
        dve(lambda e: e.tensor_scalar(out=dst_ap, in0=src_ap, scalar1=scale, scalar2=EPS, op0=ALU.mult, op1=ALU.add), rk, [tmpk])
        act(lambda e: e.activation(out=dst_ap, in_=dst_ap, func=AF.Sqrt), [tmpk], [tmpk + ("b",)])
        dve(lambda e: e.reciprocal(out=dst_ap, in_=dst_ap), [tmpk + ("b",)], wk)

    dma("sp", identf, cmat[0], writes=["c_ident"]); dma("sp", pswap, cmat[1], writes=["c_pswap"]); dma("sp", onesf, cmat[2], writes=["c_ones"])
    dma("sp", vecsb, vecs, writes=["c_vecs"])
    dma("sp", pmaskb, pmask.rearrange("a p n -> p a n"), writes=["c_pmask"])
    dma("sp", smaskb, smask.rearrange("b p n -> p b n"), writes=["c_smask"])
    dma("sp", sinkpb, sinkp, writes=["c_sinkp"]); dma("sp", sinksb, sinks_, writes=["c_sinks"])
    dve(lambda e: e.tensor_copy(out=identb, in_=identf), ["c_ident"], ["c_identb"])
    dve(lambda e: e.memset(epsb, EPS), [], ["c_eps"])
    P.barrier()

    GOFF = c.QD + 2 * KVD + 2 * CD
    UOFF = c.QD + 2 * KVD

    KSTOP = 99
    for hf in range(2):
        if KSTOP < 99 and hf == 1:
            break
        prow0 = hf * PT
        srow0 = hf * ST
        sb0 = hf * SB
        cosT = ropeb[:, 0, :]; sinT = ropeb[:, 1, :]

        tiles = [(prow0 + 128 * i, 128, 128 * i, xp) for i in range(NB + 1)] + [(srow0, ST, 128 + PT, xs)]
        for ti, (r0, nr, col0, src) in enumerate(tiles):
            xt = xt2[ti % 2]
            par = ti % 2
            hb = hb2[par]
            pb = 4 * par
            dma("sp", xt[0:nr, :], src[r0:r0 + nr, :], writes=[("xt", par)] + (["x_first"] if (hf == 0 and ti == 0) else []))
            act(lambda e, xt=xt, nr=nr, hb=hb, par=par: e.activation(out=hb[0:nr, :], in_=xt[0:nr, :], func=AF.Square, accum_out=small[0:nr, 2 * par:2 * par + 1]),
                [("xt", par)], [("hb", par), ("ss", par)])
            rsqrt_chain(small[0:nr, 2 * par:2 * par + 1], small[0:nr, 2 * par + 1:2 * par + 2], 1.0 / D, [("ss", par)], [("rstd", par)], ("rs_t", par))
            act(lambda e, xt=xt, nr=nr, hb=hb, par=par: e.activation(out=hb[0:nr, :], in_=xt[0:nr, :], func=AF.Copy, scale=small[0:nr, 2 * par + 1:2 * par + 2]),
                [("xt", par), ("rstd", par)], [("hb", par)])
            for kc in range(KC):
                b = pb + kc // 8
                j = kc % 8
                pe(lambda e, o=bank(b, j * 128, j * 128 + nr, BF16), i=hb[0:nr, kc * 128:(kc + 1) * 128], nr=nr:
                   e.transpose(o, i, identb[0:nr, 0:nr]), [("hb", par), "c_identb"], [("ps", b)])
            for g8 in range((KC + 7) // 8):
                n8 = min(8, KC - g8 * 8)
                b = pb + g8
                src_ps = PS[:, b, :].bitcast(BF16).rearrange("p (j n) -> p j n", n=128)[:, 0:n8, 0:nr]
                dve(lambda e, s_=src_ps, g8=g8, n8=n8, nr=nr, col0=col0: e.tensor_tensor(
                    out=hT[:, g8 * 8:g8 * 8 + n8, col0:col0 + nr], in0=s_,
                    in1=gmix[:, g8 * 8:g8 * 8 + n8].unsqueeze(2).to_broadcast([128, n8, nr]), op=ALU.mult),
                    [("ps", b), "c_vecs"], ["hT"])
        P.barrier()

        dma("sp", ropeb, rope[hf].rearrange("a p n -> p a n"), writes=["rope"])
        dve(lambda e: e.memset(vpad.rearrange("p a b c d -> p (a b c d)"), 0.0), [], ["vpad0"] + [("vpad", b_) for b_ in range(NBK + 1)])
        dve(lambda e: e.memset(vnpad.rearrange("p b c d -> p (b c d)"), 0.0), [], ["vnpad0"])
        nst = SB * 30
        tsz = 120 if nst % 120 == 0 else nst
        ststeps = [(t0, c4) for t0 in range(0, nst, tsz) for c4 in range(0, CC, 2)]

        def st_dma(j):
            t0, c4 = ststeps[j]
            b0 = t0 // 30; nb_ = tsz // 30; n4 = min(2, CC - c4)
            dma("sp", stt2[j % 2][0:tsz, 0:n4 * 128], sc[sb0 + b0: sb0 + b0 + nb_, :, c4 * 128:(c4 + n4) * 128].rearrange("b r c -> (b r) c"), writes=[("stt", j % 2)])

        def st_tr(j):
            t0, c4 = ststeps[j]
            b0 = t0 // 30; nb_ = tsz // 30; n4 = min(2, CC - c4)
            for jj in range(n4):
                pe(lambda e, o=bank(6, jj * 128, jj * 128 + tsz), i=stt2[j % 2][0:tsz, jj * 128:(jj + 1) * 128]:
                   e.transpose(o, i, identf[0:tsz, 0:tsz]), [("stt", j % 2), "c_ident"], [("ps", 6)])
            src_ps = PS[:, 6, :].rearrange("p (j n) -> p j n", n=128)[:, 0:n4, 0:tsz].rearrange("p j (b r) -> p j b r", r=30)
            act(lambda e, s_=src_ps, c4=c4, n4=n4, b0=b0, nb_=nb_: e.activation(
                out=cuS[:, c4:c4 + n4, b0:b0 + nb_, 0:30], in_=s_, func=AF.Copy), [("ps", 6)], ["cuS_state"])
        st_dma(0)
        if len(ststeps) > 1:
            st_dma(1)
        for j in range(len(ststeps)):
            fillers.append(lambda j=j: st_tr(j))
            if j + 2 < len(ststeps):
                fillers.append(lambda j=j: st_dma(j + 2))
        dma("sp", cso[sb0:sb0 + SB, 0:26, :], sc[sb0:sb0 + SB, 4:30, :])
        dma("sp", kso[sb0:sb0 + SB, 0:124, :], ck[sb0:sb0 + SB, 4:128, :])
        dma("sp", vso[sb0:sb0 + SB, 0:124, :], cv[sb0:sb0 + SB, 4:128, :])

        HK = [(0, NCOL // 2), (NCOL // 2, NCOL)]
        HM = [(128, 128 + PT // 2), (128 + PT // 2, NCOL)]
        HU = [(96, 96 + NU // 2), (96 + NU // 2, NCOL)]
        rhs_h = lambda kc, a, b: hT[:, kc, a:b]
        ropectr = [0]

        def rope_evac(banks, halves, dst_fn, dkeys, extra_reads=(), defer=False):
            i = ropectr[0] % 2; ropectr[0] += 1
            for hi, (a, b) in enumerate(halves):
                act(lambda e, o=zf[i][:, a:b], s_=bank(banks[hi], 0, b - a): e.activation(out=o, in_=s_, func=AF.Copy),
                    [("ps", banks[hi])], [("zf", i, hi)])

            def rest():
                for hi, (a, b) in enumerate(halves):
                    pe(lambda e, o=bank(6 + hi, 0, b - a), r=zf[i][:, a:b]: e.matmul(o, lhsT=pswap, rhs=r, start=True, stop=True),
                       [("zf", i, hi), "c_pswap"], [("ps", 6 + hi)])
                for hi, (a, b) in enumerate(halves):
                    dve(lambda e, o=t1b[i][:, a:b], z=zf[i][:, a:b], cs=cosT[:, a:b]: e.tensor_tensor(out=o, in0=z, in1=cs, op=ALU.mult),
                        [("zf", i, hi), "rope"], [("t1", i, hi)])
                    dve(lambda e, o=t2b[i][:, a:b], z=bank(6 + hi, 0, b - a), sn=sinT[:, a:b]: e.tensor_tensor(out=o, in0=z, in1=sn, op=ALU.mult),
                        [("ps", 6 + hi), "rope"], [("t2", i, hi)])
                    dve(lambda e, o=dst_fn(a, b), x1=t1b[i][:, a:b], x2=t2b[i][:, a:b]: e.tensor_tensor(out=o, in0=x1, in1=x2, op=ALU.add),
                        [("t1", i, hi), ("t2", i, hi)] + list(extra_reads), dkeys)
            if defer:
                return rest
            rest()
            return None

        def evac_k(ci, banks):
            rope_evac(banks, HK, lambda a, b, ci=ci: kTf[:, ci, a:b], [("kTf", ci)], extra_reads=["kvf_alias"])
            act(lambda e, ci=ci: e.activation(out=kTb_[:, ci, :], in_=kTf[:, ci, :], func=AF.Copy), [("kTf", ci)], [("kTb", ci)])
        for g0 in range(0, KVC, 2):
            n = min(2, KVC - g0)
            dense(w_in, 0, KC, [(c.QD + g0 * 128, n * 128)], rhs_h, ["hT"], HK, lambda ci, bk, g0=g0: evac_k(g0 + ci, bk))
        for kv in range(NKV):
            srcp = kTb_[64 * (kv % 2):64 * (kv % 2) + 64, kv // 2, :]
            for e2 in range(2):
                dma("sp", kTd[64 * e2:64 * e2 + 64, kv, :], srcp, reads=[("kTb", kv // 2)], writes=[("kTd", kv, e2)])

        def evac_v(ci, banks):
            for hi, (a, b) in enumerate(HK):
                act(lambda e, o=vTf[:, ci, a:b], s_=bank(banks[hi], 0, b - a): e.activation(out=o, in_=s_, func=AF.Copy),
                    [("ps", banks[hi]), "kvf_alias"], [("vTf", ci, hi)])
        for g0 in range(0, KVC, 2):
            n = min(2, KVC - g0)
            dense(w_in, 0, KC, [(c.QD + KVD + g0 * 128, n * 128)], rhs_h, ["hT"], HK, lambda ci, bk, g0=g0: evac_v(g0 + ci, bk))
        def v_step(ci, blk):
            nr = 128 if blk < NBK else ST
            col0 = blk * 128
            pe(lambda e, o=bank(7, 0, 128)[0:nr, :], i=vTf[:, ci, col0:col0 + nr]: e.transpose(o, i, identf),
               [("vTf", ci, 0), ("vTf", ci, 1), "c_ident"], [("ps", 7)])
            for e2 in range(2):
                src_ps = PS[0:nr, 7, 0:128].rearrange("p (k d) -> p k d", d=64)
                if blk < NBK:
                    dst = vpad[:, blk, 2 * ci:2 * ci + 2, e2, 64 * e2:64 * e2 + 64]
                else:
                    dst = vnpad[0:nr, 2 * ci:2 * ci + 2, e2, 64 * e2:64 * e2 + 64]
                (act if e2 == 0 else dve)(
                    (lambda e, o=dst, s_=src_ps: e.activation(out=o, in_=s_, func=AF.Copy)) if e2 == 0 else
                    (lambda e, o=dst, s_=src_ps: e.tensor_copy(out=o, in_=s_)),
                    [("ps", 7), "vpad0", "vnpad0"], [("vpad", blk)])
            if (blk == NBK - 1 and hf == 1) or blk == NBK:
                dve(lambda e, nr=nr: e.tensor_copy(out=tokf[0:nr, 0:128], in_=PS[0:nr, 7, 0:128]), [("ps", 7)], ["tokf"])
                if blk == NBK:
                    for b_ in range(SB):
                        dma("sp", vso[sb0 + b_, 124:128, ci * 128:(ci + 1) * 128], tokf[4 * b_:4 * b_ + 4, 0:128], reads=["tokf"])
                else:
                    dma("sp", vpo[:, ci * 128:(ci + 1) * 128], tokf[:, 0:128], reads=["tokf"])

        def k_step(ci, blk):
            nr = 128 if blk < NBK else ST
            col0 = blk * 128
            pe(lambda e, o=bank(7, 0, 128)[0:nr, :], i=kTf[:, ci, col0:col0 + nr]: e.transpose(o, i, identf),
               [("kTf", ci), "c_ident"], [("ps", 7)])
            dve(lambda e, nr=nr: e.tensor_copy(out=tokf[0:nr, 0:128], in_=PS[0:nr, 7, 0:128]), [("ps", 7)], ["tokf"])
            if blk == NBK:
                for b_ in range(SB):
                    dma("sp", kso[sb0 + b_, 124:128, ci * 128:(ci + 1) * 128], tokf[4 * b_:4 * b_ + 4, 0:128], reads=["tokf"])
            else:
                dma("sp", kpo[:, ci * 128:(ci + 1) * 128], tokf[:, 0:128], reads=["tokf"])
        for ci in range(KVC):
            for blk in range(NBK + 1):
                fillers.append(lambda ci=ci, blk=blk: v_step(ci, blk))
            for blk in ([NBK - 1] if hf == 1 else []) + [NBK]:
                fillers.append(lambda ci=ci, blk=blk: k_step(ci, blk))
        P.barrier()

        def cu_step(which, c4):
            n4 = min(4, CC - c4)
            for j in range(n4):
                cc = c4 + j
                if which == 0:
                    src = cuP[:, cc, PT:PT + 32]; nr = 32
                else:
                    src = cusn[:, cc, :]; nr = ST
                pe(lambda e, o=bank(7, j * 128, (j + 1) * 128)[0:nr, :], i=src: e.transpose(o, i, identf),
                   [("cu", cc), ("cusn", cc), "c_ident"], [("ps", 7)])
            act(lambda e, c4=c4, n4=n4, nr=nr: e.activation(out=tokc[0:nr, 0:n4 * 128], in_=PS[0:nr, 7, 0:n4 * 128], func=AF.Copy),
                [("ps", 7)], ["tokc"])
            if which == 0:
                dma("sp", cpo[:, c4 * 128:(c4 + n4) * 128], tokc[0:32, 0:n4 * 128], reads=["tokc"])
            else:
                for b_ in range(SB):
                    dma("sp", cso[sb0 + b_, 26:30, c4 * 128:(c4 + n4) * 128], tokc[4 * b_:4 * b_ + 4, 0:n4 * 128], reads=["tokc"])
        sgctr = [0]
        for g0 in range(0, CC, 2):
            si = sgctr[0] % 2; sgctr[0] += 1

            def evac_ub(ci, banks, si=si):
                for hi, (a, b) in enumerate(HU):
                    act(lambda e, o=sgb[si][:, ci, a - 96:b - 96], s_=bank(banks[hi], 0, b - a): e.activation(out=o, in_=s_, func=AF.Sigmoid),
                        [("ps", banks[hi])], [("sg", si, ci, hi), "kvf_alias"])

            def evac_ua(ci, banks, si=si, g0=g0):
                cc = g0 + ci
                (a0, b0_), (a1, b1) = HU
                n0 = b0_ - a0
                dve(lambda e: e.tensor_tensor(out=cuP[:, cc, 0:n0], in0=bank(banks[0], 0, n0), in1=sgb[si][:, ci, 0:n0], op=ALU.mult),
                    [("ps", banks[0]), ("sg", si, ci, 0)], [("cu", cc)])
                n1p = (32 + PT) - n0
                dve(lambda e: e.tensor_tensor(out=cuP[:, cc, n0:n0 + n1p], in0=bank(banks[1], 0, n1p), in1=sgb[si][:, ci, n0:n0 + n1p], op=ALU.mult),
                    [("ps", banks[1]), ("sg", si, ci, 1)], [("cu", cc)])
                dve(lambda e: e.tensor_tensor(out=cusn[:, cc, :], in0=bank(banks[1], n1p, n1p + ST),
                                              in1=sgb[si][:, ci, n0 + n1p:n0 + n1p + ST], op=ALU.mult),
                    [("ps", banks[1]), ("sg", si, ci, 1)], [("cusn", cc)])
                act(lambda e: e.activation(out=cuS[:, cc, :, 30:34], in_=cusn[:, cc, :].rearrange("p (b t) -> p b t", t=4), func=AF.Copy),
                    [("cusn", cc), "cuS_state"], [("cuSn", cc)])
            dense(w_in, 0, KC, [(UOFF + CD + g0 * 128, 256)], rhs_h, ["hT"], HU, evac_ub)
            dense(w_in, 0, KC, [(UOFF + g0 * 128, 256)], rhs_h, ["hT"], HU, evac_ua)
            if (g0 + 2) % 4 == 0 or g0 + 2 >= CC:
                c4_ = (g0 // 4) * 4
                for which in ([0] if hf == 1 else []) + [1]:
                    fillers.append(lambda which=which, c4_=c4_: cu_step(which, c4_))
        P.barrier()
        qpend = [None]

        def evac_q(qc, banks):
            rest = rope_evac(banks, HM, lambda a, b, qc=qc: qT[:, qc, a - 128:b - 128], [("qT", qc)], defer=True)
            if qpend[0] is not None:
                qpend[0]()
            qpend[0] = rest
        for g0 in range(0, QC, 2):
            dense(w_in, 0, KC, [(g0 * 128, 256)], rhs_h, ["hT"], HM, lambda ci, bk, g0=g0: evac_q(g0 + ci, bk))
        if qpend[0] is not None:
            qpend[0]()

        drain(len(fillers))
        P.barrier()
        if KSTOP <= 1:
            break

        iters = [(blk, kv, hg) for blk in range(1, NB + 1) for kv in range(NKV) for hg in range(2)]

        def emit_qk(n):
            blk, kv, hg = iters[n]
            i2 = n % 2
            bS = (0, 1) if i2 == 0 else (2, 3)
            for g4 in range(4):
                g = hg * 4 + g4; h = kv * 8 + g; qc = h // 2; e2 = h % 2
                pe(lambda e, o=bank(bS[g4 % 2], (g4 // 2) * 256, (g4 // 2) * 256 + 256),
                   l=qT[64 * e2:64 * e2 + 64, qc, (blk - 1) * 128:blk * 128],
                   r=kTd[64 * e2:64 * e2 + 64, kv, (blk - 1) * 128:(blk + 1) * 128]: e.matmul(o, lhsT=l, rhs=r, start=True, stop=True),
                   [("qT", qc), ("kTd", kv, e2)], [("ps", bS[g4 % 2])])

        def emit_A(n):
            blk, kv, hg = iters[n]
            i2 = n % 2
            bS = (0, 1) if i2 == 0 else (2, 3)
            bT = 4 + i2
            bO = 6 + i2
            mk = pmaskb[:, 0 if (hf == 0 and blk == 1) else 1, :]
            psS = PS[:, bS[0]:bS[0] + 2, :].rearrange("p b (g n) -> p (b g) n", n=256)
            dve(lambda e, o=Sp[i2], s_=psS, mk=mk: e.scalar_tensor_tensor(out=o, in0=s_, scalar=0.125, in1=mk.unsqueeze(1).to_broadcast([128, 4, 256]), op0=ALU.mult, op1=ALU.add),
                [("ps", bS[0]), ("ps", bS[1]), "c_pmask"], [("Sp", i2)])
            sm = small[:, 8 + 16 * i2: 8 + 16 * i2 + 16]
            mx = sm[:, 0:4]; ngm = sm[:, 4:8]; rs = sm[:, 8:12]; es = sm[:, 12:16]
            skp = sinkpb[:, kv * 8 + hg * 4: kv * 8 + hg * 4 + 4]
            dve(lambda e, o=mx, s_=Sp[i2]: e.reduce_max(out=o, in_=s_, axis=AX.X), [("Sp", i2)], [("mx", i2)])
            dve(lambda e, o=mx, s_=skp: e.tensor_tensor(out=o, in0=o, in1=s_, op=ALU.max), [("mx", i2), "c_sinkp"], [("mx2", i2)])
            dve(lambda e, o=ngm, s_=mx: e.tensor_scalar(out=o, in0=s_, scalar1=-1.0, scalar2=None, op0=ALU.mult), [("mx2", i2)], [("ngm", i2)])
            dve(lambda e, o=es, s_=skp, n_=ngm: e.tensor_tensor(out=o, in0=s_, in1=n_, op=ALU.add), [("ngm", i2), "c_sinkp"], [("es0", i2)])
            for g4 in range(4):
                act(lambda e, o=Pf[i2][:, g4, :], s_=Sp[i2][:, g4, :], bz=ngm[:, g4:g4 + 1], ao=rs[:, g4:g4 + 1]:
                    e.activation(out=o, in_=s_, func=AF.Exp, bias=bz, accum_out=ao), [("Sp", i2), ("ngm", i2)], [("Pf", i2, g4), ("rs", i2, g4)])
            act(lambda e, o=es: e.activation(out=o, in_=o, func=AF.Exp), [("es0", i2)], [("es", i2)])

        def emit_B(n):
            blk, kv, hg = iters[n]
            i2 = n % 2
            bT = 4 + i2
            bO = 6 + i2
            sm = small[:, 8 + 16 * i2: 8 + 16 * i2 + 16]
            mx = sm[:, 0:4]; ngm = sm[:, 4:8]; rs = sm[:, 8:12]; es = sm[:, 12:16]
            dve(lambda e, o=es, r_=rs: e.tensor_tensor(out=o, in0=o, in1=r_, op=ALU.add), [("es", i2)] + [("rs", i2, g4) for g4 in range(4)], [("den", i2)])
            dve(lambda e, o=es: e.reciprocal(out=o, in_=o), [("den", i2)], [("rinv", i2)])
            dve(lambda e, o=Pn[i2], p_=Pf[i2], r_=es: e.tensor_tensor(out=o, in0=p_, in1=r_.unsqueeze(2).to_broadcast([128, 4, 256]), op=ALU.mult),
                [("rinv", i2)] + [("Pf", i2, g4) for g4 in range(4)], [("Pn", i2)])
            for kb in range(2):
                for g4 in range(4):
                    j = kb * 4 + g4
                    pe(lambda e, o=bank(bT, j * 128, (j + 1) * 128, BF16), i=Pn[i2][:, g4, kb * 128:(kb + 1) * 128]: e.transpose(o, i, identb),
                       [("Pn", i2), "c_identb"], [("ps", bT)])
            act(lambda e, o=PTb[i2].rearrange("p a b c -> p (a b c)"), s_=bank(bT, 0, 1024, BF16): e.activation(out=o, in_=s_, func=AF.Copy),
                [("ps", bT)], [("PT", i2)])
            first = True
            for kb in range(2):
                for e2 in range(2):
                    rhs = PTb[i2][:, kb, 2 * e2:2 * e2 + 2, :].rearrange("p j q -> p (j q)")
                    pe(lambda e, o=bank(bO, 0, 256), l=vpad[:, blk - 1 + kb, kv, e2, :], r=rhs, st=first, sp_=(kb == 1 and e2 == 1):
                       e.matmul(o, lhsT=l, rhs=r, start=st, stop=sp_),
                       [("PT", i2), ("vpad", blk - 1 + kb)], [("ps", bO)])
                    first = False
            qc0 = kv * 4 + hg * 2
            act(lambda e, o=qT[:, qc0:qc0 + 2, (blk - 1) * 128:blk * 128], s_=bank(bO, 0, 256).rearrange("p (j q) -> p j q", q=128):
                e.activation(out=o, in_=s_, func=AF.Copy), [("ps", bO)], [("qT", qc0), ("qT", qc0 + 1)])

        NI = len(iters)
        emit_qk(0)
        if NI > 1:
            emit_qk(1)
        emit_A(0)
        for n in range(NI):
            if n + 2 < NI:
                emit_qk(n + 2)
            if n + 1 < NI:
                emit_A(n + 1)
            emit_B(n)
        P.barrier()
        if KSTOP <= 2:
            break
        dma("sp", qtmp, qT[64:128, :, PT:PT + ST], reads=[("qT", q_) for q_ in range(QC)], writes=["qtmp"])
        for kv in range(NKV):
            act(lambda e, kv=kv: e.activation(out=qs[:, :, kv, 0, :, :].rearrange("p b j t -> p j b t"),
                                              in_=qT[0:64, kv * 4:kv * 4 + 4, PT:PT + ST].rearrange("p j (b t) -> p j b t", t=4), func=AF.Copy),
                [("qT", q_) for q_ in range(QC)], [("qs_e", kv)])
            act(lambda e, kv=kv: e.activation(out=qs[:, :, kv, 1, :, :].rearrange("p b j t -> p j b t"),
                                              in_=qtmp[:, kv * 4:kv * 4 + 4, :].rearrange("p j (b t) -> p j b t", t=4), func=AF.Copy),
                ["qtmp"], [("qs_o", kv)])
        def s_front(b_):
            p_ = b_ % 2
            dma("sp", ckf2[p_], ck[sb0 + b_], writes=[("ckf", p_)]); dma("sp", cvf2[p_], cv[sb0 + b_], writes=[("cvf", p_)])
            act(lambda e: e.activation(out=ckb2[p_], in_=ckf2[p_], func=AF.Copy), [("ckf", p_)], [("ckb", p_)])
            for e2 in range(2):
                dve(lambda e, e2=e2: e.tensor_copy(out=cvpad2[p_][:, :, e2, 64 * e2:64 * e2 + 64], in_=cvf2[p_].rearrange("p (k d) -> p k d", d=64)),
                    [("cvf", p_)], [("cvpad", p_, e2)])
            if b_ < 2:
                dve(lambda e: e.memset(cvpad2[p_][:, :, 0, 64:128], 0.0), [], [("cvpad", p_, 0)])
                dve(lambda e: e.memset(cvpad2[p_][:, :, 1, 0:64], 0.0), [], [("cvpad", p_, 1)])
            for kv in range(NKV):
                pe(lambda e, o=bank(4, kv * 128, (kv + 1) * 128, BF16)[0:64, :], i=ckb2[p_][:, kv * 64:(kv + 1) * 64]: e.transpose(o, i, identb),
                   [("ckb", p_), "c_identb"], [("ps", 4)])
            act(lambda e: e.activation(out=kTs[:, :, 0:128], in_=PS[0:64, 4, :].bitcast(BF16)[:, 0:NKV * 128].rearrange("p (k n) -> p k n", n=128), func=AF.Copy),
                [("ps", 4)], ["kTs_c"])
            dve(lambda e: e.tensor_copy(out=kTs[:, :, 128:NK], in_=kTd[0:64, :, 128 + PT:NCOL]), [("kTd", kv_, 0) for kv_ in range(NKV)], ["kTs_n"])
            for kv in range(NKV):
                pe(lambda e, o=bank(2 * p_, kv * NK, (kv + 1) * NK)[0:32, :] if NKV * NK <= 512 else bank(2 * p_ + kv // 2, (kv % 2) * NK, (kv % 2 + 1) * NK)[0:32, :],
                   l=qs[:, b_, kv].rearrange("p e j t -> p (e j t)"), r=kTs[:, kv, :]: e.matmul(o, lhsT=l, rhs=r, start=True, stop=True),
                   [("qs_e", kv), ("qs_o", kv), "kTs_c", "kTs_n"], [("ps", 2 * p_), ("ps", 2 * p_ + 1)])

        def s_back(b_):
            p_ = b_ % 2
            if NKV * NK <= 512:
                psSs = PS[0:32, 2 * p_, 0:NKV * NK].rearrange("p (k n) -> p k n", n=NK)
            else:
                psSs = PS[0:32, 2 * p_:2 * p_ + 2, 0:2 * NK].rearrange("p b (k n) -> p b k n", n=NK)
            if NKV * NK <= 512:
                dve(lambda e, s_=psSs, b_=b_: e.scalar_tensor_tensor(out=SpS, in0=s_, scalar=0.125, in1=smaskb[:, b_, :].unsqueeze(1).to_broadcast([32, NKV, NK]), op0=ALU.mult, op1=ALU.add),
                    [("ps", 2 * p_), ("ps", 2 * p_ + 1), "c_smask"], ["SpS"])
            else:
                for bk in range(2):
                    dve(lambda e, bk=bk, b_=b_: e.scalar_tensor_tensor(out=SpS[:, 2 * bk:2 * bk + 2, :], in0=PS[0:32, 2 * p_ + bk, 0:2 * NK].rearrange("p (k n) -> p k n", n=NK), scalar=0.125,
                                                                       in1=smaskb[:, b_, :].unsqueeze(1).to_broadcast([32, 2, NK]), op0=ALU.mult, op1=ALU.add),
                        [("ps", 2 * p_ + bk), "c_smask"], ["SpS"])
            sm = small[0:32, 40:40 + 4 * NKV]
            mx = sm[:, 0:NKV]; ngm = sm[:, NKV:2 * NKV]; rs = sm[:, 2 * NKV:3 * NKV]; es = sm[:, 3 * NKV:4 * NKV]
            dve(lambda e: e.reduce_max(out=mx, in_=SpS, axis=AX.X), ["SpS"], ["smx"])
            dve(lambda e: e.tensor_tensor(out=mx, in0=mx, in1=sinksb, op=ALU.max), ["smx", "c_sinks"], ["smx2"])
            dve(lambda e: e.tensor_scalar(out=ngm, in0=mx, scalar1=-1.0, scalar2=None, op0=ALU.mult), ["smx2"], ["sngm"])
            dve(lambda e: e.tensor_tensor(out=es, in0=sinksb, in1=ngm, op=ALU.add), ["sngm", "c_sinks"], ["ses0"])
            for kv in range(NKV):
                act(lambda e, kv=kv: e.activation(out=PfS[:, kv, :], in_=SpS[:, kv, :], func=AF.Exp, bias=ngm[:, kv:kv + 1], accum_out=rs[:, kv:kv + 1]),
                    ["SpS", "sngm"], [("PfS", kv), ("srs", kv)])
            act(lambda e: e.activation(out=es, in_=es, func=AF.Exp), ["ses0"], ["ses"])
            dve(lambda e: e.tensor_tensor(out=es, in0=es, in1=rs, op=ALU.add), ["ses"] + [("srs", kv) for kv in range(NKV)], ["sden"])
            dve(lambda e: e.reciprocal(out=es, in_=es), ["sden"], ["srinv"])
            dve(lambda e: e.tensor_tensor(out=PnS, in0=PfS, in1=es.unsqueeze(2).to_broadcast([32, NKV, NK]), op=ALU.mult),
                ["srinv"] + [("PfS", kv) for kv in range(NKV)], ["PnS"])
            for kv in range(NKV):
                pe(lambda e, o=bank(5, kv * 32, (kv + 1) * 32, BF16), i=PnS[:, kv, 0:128]: e.transpose(o, i, identb[0:32, 0:32]), ["PnS", "c_identb"], [("ps", 5)])
                pe(lambda e, o=bank(5, 512 + kv * 32, 512 + (kv + 1) * 32, BF16)[0:ST, :], i=PnS[:, kv, 128:NK]: e.transpose(o, i, identb[0:32, 0:32]), ["PnS", "c_identb"], [("ps", 5)])
            act(lambda e: e.activation(out=PTc.rearrange("p k n -> p (k n)"), in_=bank(5, 0, NKV * 32, BF16), func=AF.Copy), [("ps", 5)], ["PTc"])
            act(lambda e: e.activation(out=PTn[0:ST].rearrange("p k n -> p (k n)"), in_=bank(5, 512, 512 + NKV * 32, BF16)[0:ST, :], func=AF.Copy), [("ps", 5)], ["PTn"])
            for kv in range(NKV):
                for e2 in range(2):
                    rc = PTc[:, kv, 16 * e2:16 * e2 + 16]
                    rn = PTn[0:ST, kv, 16 * e2:16 * e2 + 16]
                    ob = bank(6, kv * 16, (kv + 1) * 16)
                    pe(lambda e, o=ob, l=cvpad2[p_][:, kv, e2, :], r=rc, st=(e2 == 0): e.matmul(o, lhsT=l, rhs=r, start=st, stop=False),
                       ["PTc", ("cvpad", p_, 0), ("cvpad", p_, 1)], [("ps", 6)])
                    pe(lambda e, o=ob, l=vnpad[0:ST, kv, e2, :], r=rn, sp_=(e2 == 1): e.matmul(o, lhsT=l, rhs=r, start=False, stop=sp_),
                       ["PTn", ("vpad", NBK)], [("ps", 6)])
            act(lambda e, b_=b_: e.activation(out=qT[:, :, PT + 4 * b_:PT + 4 * b_ + 4], in_=PS[:, 6, 0:QC * 4].rearrange("p (c t) -> p c t", t=4), func=AF.Copy),
                [("ps", 6)] + [("qs_e", k_) for k_ in range(NKV)] + ["qtmp"], [("qT", q_) for q_ in range(QC)])
        s_front(0)
        for b_ in range(SB):
            if b_ + 1 < SB:
                s_front(b_ + 1)
            s_back(b_)
        P.barrier()
        if KSTOP <= 3:
            break

        allcu = [("cuSn", cc) for cc in range(CC)] + ["cuS_state"]
        sconv = []
        for k in range(CW):
            wb = wdw[:, :, k:k + 1].unsqueeze(3).to_broadcast([128, CC, SB, 4])
            if k == 0:
                sconv.append(lambda wb=wb: dve(lambda e, wb=wb: e.tensor_tensor(out=accS, in0=cuS[:, :, :, 0:4], in1=wb, op=ALU.mult), allcu + ["c_vecs"], ["accS"]))
            else:
                ti_ = k % 2
                sconv.append(lambda wb=wb, k=k, ti_=ti_: dve(lambda e, wb=wb, k=k, ti_=ti_: e.tensor_tensor(out=tmpS[ti_], in0=cuS[:, :, :, k:k + 4], in1=wb, op=ALU.mult), allcu + ["c_vecs"], [("tmpS", ti_)]))
                sconv.append(lambda ti_=ti_: dve(lambda e, ti_=ti_: e.tensor_tensor(out=accS, in0=accS, in1=tmpS[ti_], op=ALU.add), ["accS", ("tmpS", ti_)], ["accS"]))
        sci = [0]

        def emit_sconv(n):
            for _ in range(n):
                if sci[0] < len(sconv):
                    sconv[sci[0]](); sci[0] += 1
        dgctr = [0]
        for cc in range(CC):
            act(lambda e, cc=cc: e.activation(out=cub[:, cc, :], in_=cuP[:, cc, :], func=AF.Copy), [("cu", cc)], [("cub", cc)])
        for cc in range(CC):
            bconv = cc % 6
            for k0 in range(0, CW, 16):
                nk = min(16, CW - k0)
                di = dgctr[0] % 2; dgctr[0] += 1
                dve(lambda e, di=di, cc=cc, k0=k0, nk=nk: e.tensor_tensor(
                    out=dgb[di][:, 0:nk, :], in0=identb.unsqueeze(1).to_broadcast([128, nk, 128]),
                    in1=wdw[:, cc, k0:k0 + nk].unsqueeze(2).to_broadcast([128, nk, 128]), op=ALU.mult),
                    ["c_identb", "c_vecs"], [("dg", di)])
                emit_sconv(3)
                for k in range(k0, k0 + nk):
                    pe(lambda e, o=bank(bconv, 0, PT), l=dgb[di][:, k - k0, :], r=cub[:, cc, 2 + k:2 + k + PT], st=(k == 0), sp_=(k == CW - 1): e.matmul(o, lhsT=l, rhs=r, start=st, stop=sp_),
                       [("dg", di), ("cub", cc)], [("ps", bconv)])
            act(lambda e, cc=cc, bconv=bconv: e.activation(out=cuP[:, cc, 0:PT], in_=bank(bconv, 0, PT), func=AF.Identity, bias=bdw[:, cc:cc + 1]),
                [("ps", bconv), "c_vecs"], [("cu", cc), ("c", cc)])
        emit_sconv(len(sconv))
        dve(lambda e: e.tensor_tensor(out=cuP[:, :, PT:PT + ST], in0=accS.rearrange("p c b t -> p c (b t)"), in1=bdw.unsqueeze(2).to_broadcast([128, CC, ST]), op=ALU.add),
            ["accS", "c_vecs"] + [("cu", cc) for cc in range(CC)], [("cs_", 0)])
        HN = [(0, PT // 2), (PT // 2, NM)]
        for cc in range(CC):
            i = cc % 2
            act(lambda e, cc=cc, i=i: e.activation(out=sqb[i], in_=cuP[:, cc, 0:NM], func=AF.Square), [("c", cc), ("cs_", 0)], [("sqb", i)])
            for hi, (a, b) in enumerate(HN):
                pe(lambda e, o=bank(hi, 0, b - a), r=cuP[:, cc, a:b], st=(cc == 0), sp_=(cc == CC - 1): e.matmul(o, lhsT=onesf, rhs=r, start=st, stop=sp_),
                   [("c", cc), ("cs_", 0), "c_ones"], [("ps", hi)])
                pe(lambda e, o=bank(2 + hi, 0, b - a), r=sqb[i][:, a:b], st=(cc == 0), sp_=(cc == CC - 1): e.matmul(o, lhsT=onesf, rhs=r, start=st, stop=sp_),
                   [("sqb", i), "c_ones"], [("ps", 2 + hi)])
        for hi, (a, b) in enumerate(HN):
            act(lambda e, a=a, b=b, hi=hi: e.activation(out=meanb[:, a:b], in_=bank(hi, 0, b - a), func=AF.Copy, scale=1.0 / CD), [("ps", hi)], [("mean", hi)])
            act(lambda e, a=a, b=b, hi=hi: e.activation(out=msqb[:, a:b], in_=bank(hi, 0, b - a), func=AF.Square, scale=1.0 / CD), [("ps", hi)], [("msq", hi)])
            dve(lambda e, a=a, b=b, hi=hi: e.scalar_tensor_tensor(out=rstdb[:, a:b], in0=bank(2 + hi, 0, b - a), scalar=1.0 / CD, in1=msqb[:, a:b], op0=ALU.mult, op1=ALU.subtract),
                [("ps", 2 + hi), ("msq", hi)], [("var", hi)])
            rsqrt_chain(rstdb[:, a:b], rstdb[:, a:b], 1.0, [("var", hi)], [("lrstd", hi)], ("lv", hi))
        for cc in range(CC):
            i = cc % 2
            dve(lambda e, cc=cc, i=i: e.tensor_tensor(out=lnt[i], in0=cuP[:, cc, 0:NM], in1=meanb, op=ALU.subtract), [("c", cc), ("cs_", 0), ("mean", 0), ("mean", 1)], [("lnt", i)])
            dve(lambda e, i=i: e.tensor_tensor(out=lnt[i], in0=lnt[i], in1=rstdb, op=ALU.mult), [("lnt", i), ("lrstd", 0), ("lrstd", 1)], [("lnt2", i)])
            act(lambda e, cc=cc, i=i: e.activation(out=cT[:, cc, :], in_=lnt[i], func=AF.Silu, scale=gln[:, cc:cc + 1], bias=bln[:, cc:cc + 1]),
                [("lnt2", i), "c_vecs"], [("cT", cc), ("lnt", i)])
        P.barrier()
        if KSTOP <= 4:
            break

        HM0 = [(0, PT // 2), (PT // 2, NM)]
        rhs_hm = lambda kc, a, b: hT[:, kc, 128 + a:128 + b]
        rhs_o = lambda kc, a, b: qT[:, kc, a:b]
        rhs_c = lambda kc, a, b: cT[:, kc, a:b]
        allq = [("qT", q_) for q_ in range(QC)]; allc = [("cT", cc) for cc in range(CC)]
        for d0 in range(0, KC, 2):
            def ev_sig(dst):
                def f(ci, banks):
                    for hi, (a, b) in enumerate(HM0):
                        act(lambda e, o=dst[:, ci, a:b], s_=bank(banks[hi], 0, b - a): e.activation(out=o, in_=s_, func=AF.Sigmoid),
                            [("ps", banks[hi])], [(id(dst), ci, hi)])
                return f

            def ev_mul(dst, sg):
                def f(ci, banks):
                    for hi, (a, b) in enumerate(HM0):
                        dve(lambda e, o=dst[:, ci, a:b], s_=bank(banks[hi], 0, b - a), g_=sg[:, ci, a:b]: e.tensor_tensor(out=o, in0=s_, in1=g_, op=ALU.mult),
                            [("ps", banks[hi]), (id(sg), ci, hi)], [(id(dst), ci, hi)])
                return f

            def ev_fin(ci, banks, d0=d0):
                ev_mul(tCb, sgC)(ci, banks)
                for hi, (a, b) in enumerate(HM0):
                    dve(lambda e, o=mT[:, d0 + ci, a:b], x1=tAb[:, ci, a:b], x2=tCb[:, ci, a:b]: e.tensor_tensor(out=o, in0=x1, in1=x2, op=ALU.add),
                        [(id(tAb), ci, hi), (id(tCb), ci, hi)], [("mT", d0 + ci)])
            dense(w_in, 0, KC, [(GOFF + d0 * 128, 256)], rhs_hm, ["hT"], HM0, ev_sig(sgA))
            dense(w_ao, 0, QC, [(d0 * 128, 256)], rhs_o, allq, HM0, ev_mul(tAb, sgA))
            dense(w_in, 0, KC, [(GOFF + D + d0 * 128, 256)], rhs_hm, ["hT"], HM0, ev_sig(sgC))
            dense(w_co, 0, CC, [(d0 * 128, 256)], rhs_c, allc, HM0, ev_fin)
        P.barrier()
        if KSTOP <= 5:
            break

        rhs_m = lambda kc, a, b: mT[:, kc, a:b]
        allm = [("mT", k_) for k_ in range(KC)]
        xctr = [0]
        for d0 in range(0, KC, 2):
            xi = xctr[0] % 2; xctr[0] += 1
            dma("sp", xcb[xi][:, 0:NB, :], xp[prow0 + 128: prow0 + 128 + PT, d0 * 128:d0 * 128 + 256].rearrange("(b p) c -> p b c", p=128), writes=[("xcb", xi, 0)])
            dma("sp", xcb[xi][0:ST, NB, :], xs[srow0:srow0 + ST, d0 * 128:d0 * 128 + 256], writes=[("xcb", xi, 1)])

            def extra(ci, banks, xi=xi):
                n0 = PT // 2
                for tt in range(NT):
                    nr = 128 if tt < NB else ST
                    colm = tt * 128
                    hi = 0 if colm < n0 else 1
                    off = colm - (0 if hi == 0 else n0)
                    pe(lambda e, o=bank(banks[hi], off, off + nr), l=xcb[xi][0:nr, tt, ci * 128:(ci + 1) * 128], r=identf[0:nr, 0:nr], sp_=(tt == NT - 1 or tt == NB // 2 - 1):
                       e.matmul(o, lhsT=l, rhs=r, start=False, stop=sp_, skip_group_check=(not sp_)),
                       [("xcb", xi, 0), ("xcb", xi, 1), "c_ident"], [("ps", banks[hi])])

            def ev_x(ci, banks, d0=d0):
                dd = d0 + ci
                for hi, (a, b) in enumerate(HM0):
                    act(lambda e, o=X[:, dd, a:b], s_=bank(banks[hi], 0, b - a): e.activation(out=o, in_=s_, func=AF.Copy), [("ps", banks[hi])], [("X", dd, hi)])
                i = dd % 2
                act(lambda e, dd=dd, i=i: e.activation(out=sq2[i], in_=X[:, dd, :], func=AF.Square), [("X", dd, 0), ("X", dd, 1)], [("sq2", i)])
                for hi, (a, b) in enumerate(HM0):
                    pe(lambda e, o=bank(6 + hi, 0, b - a), r=sq2[i][:, a:b], st=(dd == 0), sp_=(dd == KC - 1): e.matmul(o, lhsT=onesf, rhs=r, start=st, stop=sp_),
                       [("sq2", i), "c_ones"], [("ps", 6 + hi)])
            dense(w_out, 0, KC, [(d0 * 128, 256)], rhs_m, allm, HM0, ev_x, extra_fn=extra)
        for hi, (a, b) in enumerate(HM0):
            rsqrt_chain(bank(6 + hi, 0, b - a), rstd2[:, a:b], 1.0 / D, [("ps", 6 + hi)], [("rstd2", hi)], ("r2", hi))
        P.barrier()
        for dd in range(KC):
            dve(lambda e, dd=dd: e.scalar_tensor_tensor(out=h2T[:, dd, :], in0=X[:, dd, :], scalar=gffn[:, dd:dd + 1], in1=rstd2, op0=ALU.mult, op1=ALU.mult),
                [("X", dd, 0), ("X", dd, 1), ("rstd2", 0), ("rstd2", 1), "c_vecs"], ["h2T"])

        rhs_h2 = lambda kc, a, b: h2T[:, kc, a:b]
        rhs_a = lambda kc, a, b: aT[:, kc, a:b]
        fctr = [0]
        for fh in range(2):
            f0 = fh * FCH
            nf = min(FCH, FC - f0)
            for j0 in range(0, nf, 2):
                nj = min(2, nf - j0)
                si = fctr[0] % 2; fctr[0] += 1

                def ev_g(ci, banks, si=si):
                    for hi, (a, b) in enumerate(HM0):
                        act(lambda e, o=sgF[si][:, ci, a:b], s_=bank(banks[hi], 0, b - a): e.activation(out=o, in_=s_, func=AF.Silu), [("ps", banks[hi])], [("sgF", si, ci, hi)])

                def ev_u(ci, banks, si=si, j0=j0):
                    for hi, (a, b) in enumerate(HM0):
                        dve(lambda e, o=aT[:, j0 + ci, a:b], s_=bank(banks[hi], 0, b - a), g_=sgF[si][:, ci, a:b]: e.tensor_tensor(out=o, in0=s_, in1=g_, op=ALU.mult),
                            [("ps", banks[hi]), ("sgF", si, ci, hi)], [("aT", j0 + ci)])
                dense(w_fi, 0, KC, [((f0 + j0) * 128, nj * 128)], rhs_h2, ["h2T"], HM0, ev_g)
                dense(w_fi, 0, KC, [(DFF + (f0 + j0) * 128, nj * 128)], rhs_h2, ["h2T"], HM0, ev_u)
            alla = [("aT", j) for j in range(nf)]
            for d0 in range(0, KC, 2):
                def ev_o(ci, banks, d0=d0, fh=fh):
                    dd = d0 + ci
                    for hi, (a, b) in enumerate(HM0):
                        dve(lambda e, o=X[:, dd, a:b], s_=bank(banks[hi], 0, b - a): e.tensor_tensor(out=o, in0=s_, in1=o, op=ALU.add),
                            [("ps", banks[hi]), ("X", dd, hi)], [("X", dd, hi)])
                    if fh == 1:
                        i = dd % 2
                        act(lambda e, dd=dd, i=i: e.activation(out=sq3[i], in_=X[:, dd, :], func=AF.Square), [("X", dd, 0), ("X", dd, 1)], [("sq3", i)])
                        for hi, (a, b) in enumerate(HM0):
                            pe(lambda e, o=bank(6 + hi, 0, b - a), r=sq3[i][:, a:b], st=(dd == 0), sp_=(dd == KC - 1): e.matmul(o, lhsT=onesf, rhs=r, start=st, stop=sp_),
                               [("sq3", i), "c_ones"], [("ps", 6 + hi)])
                dense(w_fo, f0 * 128, nf, [(d0 * 128, 256)], rhs_a, alla, HM0, ev_o)
        for hi, (a, b) in enumerate(HM0):
            rsqrt_chain(bank(6 + hi, 0, b - a), rstd3[:, a:b], 1.0 / D, [("ps", 6 + hi)], [("rstd3", hi)], ("r3", hi))
        for dd in range(KC):
            dve(lambda e, dd=dd: e.scalar_tensor_tensor(out=X[:, dd, :], in0=X[:, dd, :], scalar=gfin[:, dd:dd + 1], in1=rstd3, op0=ALU.mult, op1=ALU.mult),
                [("X", dd, 0), ("X", dd, 1), ("rstd3", 0), ("rstd3", 1), "c_vecs"], [("Y", dd)])
        for tt in range(NT):
            nr = 128 if tt < NB else ST
            yi = tt % 2
            for d4 in range(0, KC, 4):
                b = (d4 // 4) % 6
                for j in range(4):
                    pe(lambda e, o=bank(b, j * 128, (j + 1) * 128)[0:nr, :], i=X[:, d4 + j, tt * 128:tt * 128 + nr]: e.transpose(o, i, identf),
                       [("Y", d4 + j), "c_ident"], [("ps", b)])
                if (d4 // 4) % 2 == 0:
                    act(lambda e, b=b, d4=d4, nr=nr, yi=yi: e.activation(out=ytok[yi][0:nr, d4 * 128:(d4 + 4) * 128], in_=PS[0:nr, b, :], func=AF.Copy), [("ps", b)], [("ytok", yi, d4)])
                else:
                    dve(lambda e, b=b, d4=d4, nr=nr, yi=yi: e.tensor_copy(out=ytok[yi][0:nr, d4 * 128:(d4 + 4) * 128], in_=PS[0:nr, b, :]), [("ps", b)], [("ytok", yi, d4)])
            if tt < NB:
                dst = yp[hf * PT + tt * 128: hf * PT + tt * 128 + 128, :]
            else:
                dst = ys[srow0:srow0 + ST, :]
            dma("sp", dst, ytok[yi][0:nr, :], reads=[("ytok", yi, d4) for d4 in range(0, KC, 4)])
        P.barrier()

    fin = P.add("sp", None, reads=[], writes=[P.phase_key], phase=False)
    for k_, last in P.dma_last.items():
        if k_[0] == "sp" and last is not fin and last not in fin.deps:
            fin.deps.append(last)

    names = ["pe", "act", "dve"] + [("sp", i) for i in range(P.dma_pool["sp"])] + [("gq", i) for i in range(P.dma_pool["gq"])]
    sem_cms = [nc.semaphore("s_%s" % (n if isinstance(n, str) else "%s%d" % n)) for n in names]
    sems = {n: cm.__enter__() for n, cm in zip(names, sem_cms)}
    P.finalize(nc, sems)
    with nc.Block() as block:
        @block.tensor
        def _(e):
            P.emit("pe", e, sems)

        @block.scalar
        def _(e):
            P.emit("act", e, sems)

        @block.vector
        def _(e):
            P.emit("dve", e, sems)

        @block.sync
        def _(e):
            P.emit("sp", e, sems)

        @block.gpsimd
        def _(e):
            P.emit("gq", e, sems)
    for cm in reversed(sem_cms):
        cm.__exit__(None, None, None)
    ps_cm.__exit__(None, None, None)
    arena_cm.__exit__(None, None, None)
    return nc, P


def _consts(cfg):
    ident = np.eye(128, dtype=np.float32)
    psw = np.zeros((128, 128), np.float32)
    for m in range(128):
        k = m + 32 if (m % 64) < 32 else m - 32
        psw[k, m] = 1.0
    ones = np.ones((128, 128), np.float32)
    return np.stack([ident, psw, ones])


def _rope_tables(pos):
    half = 32
    inv = (np.float32(10000.0) ** (-np.arange(half, dtype=np.float32) / np.float32(half))).astype(np.float32)
    ang = pos.astype(np.float32)[None, :] * inv[:, None]
    cs = np.cos(ang).astype(np.float32); sn = np.sin(ang).astype(np.float32)
    cosT = np.concatenate([cs, cs, cs, cs], 0)
    sinT = np.concatenate([-sn, sn, -sn, sn], 0)
    return np.stack([cosT, sinT])


def make_in_maps(cfg, inp):
    c = cfg
    x_prompt = np.asarray(inp["x_prompt"], np.float32)[0]
    x_sample = np.asarray(inp["x_sample"], np.float32).reshape(c.DEC_BATCH * 4, c.D)
    ckv = np.asarray(inp["cache_k"], np.float32)[0].reshape(c.DEC_BATCH, 128, c.KVD)
    cvv = np.asarray(inp["cache_v"], np.float32)[0].reshape(c.DEC_BATCH, 128, c.KVD)
    scv = np.asarray(inp["state_conv"], np.float32)[0]
    t = lambda v, n: np.ascontiguousarray(np.asarray(v, np.float32).reshape(n, 128).T)
    wdw = np.asarray(inp["w_dw"], np.float32)[0]
    wdwT = np.ascontiguousarray(wdw.reshape(CW, c.CC, 128).transpose(2, 1, 0)).reshape(128, c.CC * CW)
    vecs = np.concatenate([t(inp["g_mix_norm"], c.KC), t(inp["g_ffn_norm"], c.KC), t(inp["g_final"], c.KC), wdwT,
                           t(inp["b_dw"], c.CC), t(inp["g_conv_ln"], c.CC), t(inp["b_conv_ln"], c.CC)], 1).astype(np.float32)
    sinks = np.asarray(inp["sinks"], np.float32)[0]
    perm = np.arange(c.NH).reshape(-1, 2, 2).transpose(0, 2, 1).reshape(-1)
    sinkp = np.ascontiguousarray(np.broadcast_to(sinks[perm][None, :], (128, c.NH)))
    sk = sinks.reshape(c.NKV, 4, 2)
    sinkS = np.ascontiguousarray(np.repeat(sk.transpose(2, 1, 0).reshape(8, c.NKV), 4, axis=0))
    qi = np.arange(128)[:, None] + 128; kj = np.arange(256)[None, :]
    diff = qi - kj
    m_norm = np.where((diff >= 0) & (diff < 128), 0.0, -1e30).astype(np.float32)
    m_first = np.where((diff >= 0) & (diff < 128) & (kj >= 128), 0.0, -1e30).astype(np.float32)
    smask = np.full((c.SB, 32, c.NK), -1e30, np.float32)
    for b in range(c.SB):
        for g in range(8):
            for tq in range(4):
                r = g * 4 + tq
                smask[b, r, tq + 1:128] = 0.0
                for t2 in range(tq + 1):
                    smask[b, r, 128 + 4 * b + t2] = 0.0
    cm = _consts(c)
    W = {k: np.asarray(inp[k], np.float32)[0] for k in ("w_in", "w_attn_o", "w_conv_o", "w_out", "w_ffn_in", "w_ffn_out")}
    PTC = 2 * c.PT
    maps = []
    for core in range(c.NCORES):
        xpc = np.zeros((PTC + 128, c.D), np.float32)
        lo = core * PTC - 128
        if lo >= 0:
            xpc[:] = x_prompt[lo:lo + PTC + 128]
        else:
            xpc[128:] = x_prompt[0:PTC]
        ropes = []
        for hf in range(2):
            p0 = core * PTC + hf * c.PT
            pos = np.concatenate([np.arange(p0 - 128, p0 + c.PT), np.tile(np.arange(c.SEQ if False else PAST_LEN_, PAST_LEN_ + 4), c.SB)])
            ropes.append(_rope_tables(pos))
        b0 = core * 2 * c.SB
        maps.append({
            "xp": xpc, "xs": np.ascontiguousarray(x_sample[b0 * 4:(b0 + 2 * c.SB) * 4]),
            "ck": np.ascontiguousarray(ckv[b0:b0 + 2 * c.SB]), "cv": np.ascontiguousarray(cvv[b0:b0 + 2 * c.SB]),
            "sc": np.ascontiguousarray(scv[b0:b0 + 2 * c.SB]),
            "w_in": W["w_in"], "w_ao": W["w_attn_o"], "w_co": W["w_conv_o"], "w_out": W["w_out"], "w_fi": W["w_ffn_in"], "w_fo": W["w_ffn_out"],
            "vecs": vecs, "rope": np.stack(ropes).astype(np.float32),
            "pmask": np.stack([m_first if core == 0 else m_norm, m_norm]), "smask": smask,
            "sinkp": sinkp, "sinks": sinkS, "cmat": cm,
        })
    return maps


PAST_LEN_ = 8192
_CACHE = {}


def run(cfg, inp, trace=False, cores=None):
    key = (cfg.D, cfg.SEQ, cfg.DEC_BATCH, cfg.NCORES)
    if key not in _CACHE:
        _CACHE[key] = build_program(cfg)[0]
    nc = _CACHE[key]
    maps = make_in_maps(cfg, inp)
    if cores is not None:
        maps = [maps[i] for i in cores]
        res = run_bass_kernel_spmd(nc, maps, core_ids=list(range(len(maps))))
        return None, res
    res = run_bass_kernel_spmd(nc, maps, core_ids=list(range(cfg.NCORES)), **({"trace": True} if trace else {}))
    R = res.results
    c = cfg
    y_prompt = np.concatenate([r["yp"] for r in R], 0)[None]
    y_sample = np.concatenate([r["ys"] for r in R], 0).reshape(c.DEC_BATCH, 4, c.D)
    last = R[-1]
    k_prompt = last["kp"].reshape(1, 1, 128, c.NKV, 64)
    v_prompt = last["vp"].reshape(1, 1, 128, c.NKV, 64)
    conv_prompt = last["cp"][2:32].reshape(1, 1, 30, c.CD)
    k_sample = np.concatenate([r["ks"] for r in R], 0).reshape(1, c.DEC_BATCH, 128, c.NKV, 64)
    v_sample = np.concatenate([r["vs"] for r in R], 0).reshape(1, c.DEC_BATCH, 128, c.NKV, 64)
    conv_sample = np.concatenate([r["cs"] for r in R], 0).reshape(1, c.DEC_BATCH, 30, c.CD)
    outs = (y_prompt, y_sample, k_prompt, v_prompt, conv_prompt, k_sample, v_sample, conv_sample)
    return tuple(np.ascontiguousarray(o, dtype=np.float32) for o in outs), res


def kernel(**inputs):
    cfg = Cfg()
    outs, _ = run(cfg, inputs)
    return outs
```
